# Optimizing a Trainium2 kernel written in Bass

```python
import math, functools
import jax, jax.numpy as jnp
from jax import lax
import numpy as np

D_MODEL = 1024
BATCH = 8
SEQ = 2048
DEPTH = 4
DEC_BATCH = 32
DEC_SEQ = 32
PAST_LEN = 2048

CHUNK = 64
HEAD_DIM = 64
A_WIDTH = 256
A_GROUPS = 4
A_GROUP_DIM = A_WIDTH // A_GROUPS
A_CHUNK = 128
B_HEADS = 8
B_KV_HEADS = 2
B_GROUP = B_HEADS // B_KV_HEADS
B_WIDTH = B_HEADS * HEAD_DIM
B_KV_WIDTH = B_KV_HEADS * HEAD_DIM
B_WINDOW = 128
B_PREV = B_WINDOW // CHUNK
C_HEADS = 4
C_WIDTH = C_HEADS * HEAD_DIM
C_PREV = 8
C_REACH = C_PREV * CHUNK
C_REL_CLIP = 128
T5_BUCKETS = 32
T5_MAX_DIST = 128

MIX_WIDTH = A_WIDTH + B_WIDTH + C_WIDTH
PROJ_SIZES = [A_WIDTH, A_WIDTH, A_WIDTH,
              B_WIDTH, B_KV_WIDTH, B_KV_WIDTH, B_WIDTH,
              C_WIDTH, C_WIDTH, C_WIDTH, C_WIDTH]
PROJ_SPLITS = [int(s) for s in np.cumsum(PROJ_SIZES)[:-1]]
IN_WIDTH = int(sum(PROJ_SIZES))
RMS_EPS = 1e-6
NEG_INF = -1e30

kernel_name = "hybrid_streaming_encoder_step"


def rmsnorm(x, g):
    x32 = x.astype(jnp.float32)
    y = x32 * lax.rsqrt(jnp.mean(x32 * x32, axis=-1, keepdims=True) + RMS_EPS)
    return (y * g.astype(jnp.float32)).astype(x.dtype)


def t5_bucket(rel):
    half = T5_BUCKETS // 2
    max_exact = half // 2
    ret = jnp.where(rel > 0, half, 0)
    n = jnp.abs(rel)
    nf = jnp.maximum(n, 1).astype(jnp.float32)
    large = max_exact + (jnp.log(nf / max_exact) / math.log(T5_MAX_DIST / max_exact)
                         * (half - max_exact)).astype(jnp.int32)
    large = jnp.minimum(large, half - 1)
    return ret + jnp.where(n < max_exact, n, large)


def t5_bias_block(table, rel):
    b = table[t5_bucket(rel)].astype(jnp.float32)
    tq, l = rel.shape
    return jnp.transpose(b, (2, 0, 1)).reshape(B_KV_HEADS, B_GROUP, tq, l)


def c_bias_block(table, rel):
    idx = jnp.clip(rel, -C_REL_CLIP, C_REL_CLIP) + C_REL_CLIP
    b = table[idx].astype(jnp.float32)
    return jnp.transpose(b, (2, 0, 1))[:, None]


def band_rel(tq, length, n_past):
    return jnp.arange(length)[None, :] - n_past - jnp.arange(tq)[:, None]


def band_gather(x, n_prev):
    s = x.shape[1]
    n = s // CHUNK
    xp = jnp.pad(x, ((0, 0), (n_prev * CHUNK, 0), (0, 0), (0, 0)))
    idx = jnp.arange(n)[:, None] * CHUNK + jnp.arange((n_prev + 1) * CHUNK)[None, :]
    return xp[:, idx]


def band_valid(n, n_prev):
    length = (n_prev + 1) * CHUNK
    return (jnp.arange(n)[:, None] * CHUNK - n_prev * CHUNK + jnp.arange(length)[None, :]) >= 0


def block_attention(q, k, v, bias, valid, sinks):
    s = jnp.einsum('bnqhgd,bnlhd->bnhgql', q, k).astype(jnp.float32) * (HEAD_DIM ** -0.5) + bias
    if valid is not None:
        s = jnp.where(valid[None, :, None, None, None, :], s, NEG_INF)
    if sinks is None:
        p = jax.nn.softmax(s, axis=-1)
    else:
        sink = sinks.astype(jnp.float32).reshape(bias.shape[0], bias.shape[1], 1, 1)
        m = jnp.maximum(jnp.max(s, axis=-1, keepdims=True), sink)
        e = jnp.exp(s - m)
        p = e / (jnp.sum(e, axis=-1, keepdims=True) + jnp.exp(sink - m))
    return jnp.einsum('bnhgql,bnlhd->bnqhgd', p.astype(v.dtype), v)


def band_attention_prompt(q, k, v, n_prev, kv_heads, bias_fn, sinks, keep):
    bn, s, _ = q.shape
    n = s // CHUNK
    group = q.shape[-1] // (kv_heads * HEAD_DIM)
    qb = q.reshape(bn, n, CHUNK, kv_heads, group, HEAD_DIM)
    kh = k.reshape(bn, s, kv_heads, HEAD_DIM)
    vh = v.reshape(bn, s, kv_heads, HEAD_DIM)
    length = (n_prev + 1) * CHUNK
    bias = bias_fn(band_rel(CHUNK, length, n_prev * CHUNK))
    o = block_attention(qb, band_gather(kh, n_prev), band_gather(vh, n_prev),
                        bias, band_valid(n, n_prev), sinks)
    keep = min(keep, s)
    return o.reshape(bn, s, -1), kh[:, s - keep:], vh[:, s - keep:]


def band_attention_sample(q, k, v, cache_k, cache_v, kv_heads, bias_fn, sinks):
    bn, t, _ = q.shape
    group = q.shape[-1] // (kv_heads * HEAD_DIM)
    qb = q.reshape(bn, 1, t, kv_heads, group, HEAD_DIM)
    kn = k.reshape(bn, t, kv_heads, HEAD_DIM)
    vn = v.reshape(bn, t, kv_heads, HEAD_DIM)
    kk = jnp.concatenate([cache_k.astype(kn.dtype), kn], axis=1)[:, None]
    vv = jnp.concatenate([cache_v.astype(vn.dtype), vn], axis=1)[:, None]
    lc = cache_k.shape[1]
    bias = bias_fn(band_rel(t, lc + t, lc))
    o = block_attention(qb, kk, vv, bias, None, sinks)
    return o.reshape(bn, t, -1), kn, vn


def mixer_a(au, av, az, g_v, ws, bs, chunk_len):
    bn, s, _ = au.shape
    n = s // chunk_len
    vn = rmsnorm(av, g_v)
    shp = (bn, n, chunk_len, A_GROUPS, A_GROUP_DIM)
    w = jnp.tril(ws[:, :chunk_len, :chunk_len])
    mix = (jnp.einsum('gts,bnsgd->bntgd', w, vn.reshape(shp))
           + jnp.transpose(bs[:, :chunk_len])[None, None, :, :, None])
    y = (au.reshape(shp) * mix).reshape(bn, s, A_WIDTH)
    return y * jax.nn.silu(az), vn


def mixer_inputs(x, g_pre, w_in):
    h = rmsnorm(x, g_pre)
    return jnp.split(h @ w_in, PROJ_SPLITS, axis=-1)


def mixer_output(x, ya, yb, yc, w_out, g_post):
    y = jnp.concatenate([ya, yb, yc], axis=-1) @ w_out
    return x + rmsnorm(y, g_post)


def setup_inputs(seed: int = 0) -> dict:
    key = jax.random.key(seed)
    ks = jax.random.split(key, 18)
    b_keep = min(B_WINDOW, PAST_LEN)
    c_keep = min(C_REACH, PAST_LEN)
    nrm = jax.random.normal
    f32 = jnp.float32
    return {
        "x_prompt": nrm(ks[0], (BATCH, SEQ, D_MODEL), f32),
        "x_sample": nrm(ks[1], (DEC_BATCH, DEC_SEQ, D_MODEL), f32),
        "cache_b_k": nrm(ks[2], (DEPTH, DEC_BATCH, b_keep, B_KV_HEADS, HEAD_DIM), f32),
        "cache_b_v": nrm(ks[3], (DEPTH, DEC_BATCH, b_keep, B_KV_HEADS, HEAD_DIM), f32),
        "cache_c_k": nrm(ks[4], (DEPTH, DEC_BATCH, c_keep, C_HEADS, HEAD_DIM), f32),
        "cache_c_v": nrm(ks[5], (DEPTH, DEC_BATCH, c_keep, C_HEADS, HEAD_DIM), f32),
        "g_pre": 1.0 + 0.05 * nrm(ks[6], (DEPTH, D_MODEL), f32),
        "g_post": 1.0 + 0.05 * nrm(ks[7], (DEPTH, D_MODEL), f32),
        "w_in": nrm(ks[8], (DEPTH, D_MODEL, IN_WIDTH), f32) * D_MODEL ** -0.5,
        "w_out": nrm(ks[9], (DEPTH, MIX_WIDTH, D_MODEL), f32) * MIX_WIDTH ** -0.5,
        "a_norm_g": 1.0 + 0.05 * nrm(ks[10], (DEPTH, A_WIDTH), f32),
        "a_ws": nrm(ks[11], (DEPTH, A_GROUPS, A_CHUNK, A_CHUNK), f32) * A_CHUNK ** -0.5,
        "a_bs": 1.0 + 0.05 * nrm(ks[12], (DEPTH, A_GROUPS, A_CHUNK), f32),
        "b_sinks": nrm(ks[13], (DEPTH, B_HEADS), f32),
        "c_rel_bias": 0.1 * nrm(ks[14], (DEPTH, 2 * C_REL_CLIP + 1, C_HEADS), f32),
        "t5_bias": 0.1 * nrm(ks[15], (T5_BUCKETS, B_HEADS), f32),
    }


def reference(x_prompt, x_sample, cache_b_k, cache_b_v, cache_c_k, cache_c_v,
              g_pre, g_post, w_in, w_out, a_norm_g, a_ws, a_bs, b_sinks,
              c_rel_bias, t5_bias):
    t5_fn = functools.partial(t5_bias_block, t5_bias)
    xp, xs = x_prompt, x_sample
    bkp, bvp, ckp, cvp = [], [], [], []
    bks, bvs, cks, cvs, avs = [], [], [], [], []
    for l in range(DEPTH):
        c_fn = functools.partial(c_bias_block, c_rel_bias[l])
        au, av, az, bq, bk, bv, bz, cq, ck, cv, cz = mixer_inputs(xp, g_pre[l], w_in[l])
        ya, _ = mixer_a(au, av, az, a_norm_g[l], a_ws[l], a_bs[l], A_CHUNK)
        yb, kb, vb = band_attention_prompt(bq, bk, bv, B_PREV, B_KV_HEADS, t5_fn, b_sinks[l], B_WINDOW)
        yc, kc, vc = band_attention_prompt(cq, ck, cv, C_PREV, C_HEADS, c_fn, None, C_REACH)
        xp = mixer_output(xp, ya, yb * jax.nn.silu(bz), yc * jax.nn.silu(cz), w_out[l], g_post[l])
        bkp.append(kb); bvp.append(vb); ckp.append(kc); cvp.append(vc)
        au, av, az, bq, bk, bv, bz, cq, ck, cv, cz = mixer_inputs(xs, g_pre[l], w_in[l])
        ya, va = mixer_a(au, av, az, a_norm_g[l], a_ws[l], a_bs[l], xs.shape[1])
        yb, kb, vb = band_attention_sample(bq, bk, bv, cache_b_k[l], cache_b_v[l], B_KV_HEADS, t5_fn, b_sinks[l])
        yc, kc, vc = band_attention_sample(cq, ck, cv, cache_c_k[l], cache_c_v[l], C_HEADS, c_fn, None)
        xs = mixer_output(xs, ya, yb * jax.nn.silu(bz), yc * jax.nn.silu(cz), w_out[l], g_post[l])
        bks.append(kb); bvs.append(vb); cks.append(kc); cvs.append(vc); avs.append(va)
    return (xp, xs,
            jnp.stack(bkp), jnp.stack(bvp), jnp.stack(ckp), jnp.stack(cvp),
            jnp.stack(bks), jnp.stack(bvs), jnp.stack(cks), jnp.stack(cvs), jnp.stack(avs))
```

```python
import contextlib
import math
import os

import numpy as np

import concourse.bass as bass
import concourse.mybir as mybir
from concourse.alu_op_type import AluOpType as ALU
from concourse.bass_utils import run_bass_kernel_spmd

F32 = mybir.dt.float32
BF16 = mybir.dt.bfloat16
AF = mybir.ActivationFunctionType

DEPTH = 4
NT = 17
EPS = 1e-6
NEG = -30000.0


class Plan:
    ENG = ["pe", "act", "dve", "pool", "sp"]

    def __init__(self, nc):
        self.nc = nc
        self.stream = {e: [] for e in self.ENG}
        self.cnt = {e: 0 for e in self.ENG}
        self.waited = {e: {} for e in self.ENG}
        self.last_w = {}
        self.readers = {}
        self.dma_cnt = {}
        self.sems = {}
        self.out_tokens = []
        self.open = {}

    def _deps(self, eng, reads, writes, skip_sem=None, is_dma=False):
        deps = []
        for k in reads:
            t = self.last_w.get(k)
            if t is not None:
                deps.append((t, "raw"))
        for k in writes:
            t = self.last_w.get(k)
            if t is not None:
                deps.append((t, "waw"))
            for t in self.readers.get(k, ()):
                deps.append((t, "war"))
        need = {}
        for tok, kind in deps:
            if skip_sem is not None and tok[0] == skip_sem:
                continue
            assert not any(tok is o for o in self.open.values()), ("dependency on open DMA batch", tok)
            sem, val = tok
            if sem == eng and not is_dma and eng == "pe":
                continue
            need[sem] = max(need.get(sem, 0), val)
        waits = []
        for sem, val in need.items():
            if self.waited[eng].get(sem, 0) >= val:
                continue
            self.waited[eng][sem] = val
            waits.append((sem, val))
        return waits

    def _commit(self, token, reads, writes):
        for k in writes:
            self.last_w[k] = token
            self.readers[k] = []
        for k in reads:
            if k in writes:
                continue
            self.readers.setdefault(k, []).append(token)

    def op(self, eng, fn, reads=(), writes=()):
        reads, writes = tuple(reads), tuple(writes)
        waits = self._deps(eng, reads, writes)
        self.cnt[eng] += 1
        token = [eng, self.cnt[eng]]
        self.stream[eng].append((waits, fn, None))
        self._commit(token, reads, writes)

    def close_batch(self, sem):
        self.open.pop(sem, None)

    def dma(self, eng, sem, fn, n, reads=(), writes=(), out=False, batch=False):
        reads, writes = tuple(reads), tuple(writes)
        waits = self._deps(eng, reads, writes, skip_sem=sem, is_dma=True)
        self.dma_cnt[sem] = self.dma_cnt.get(sem, 0) + 16 * n
        if batch:
            token = self.open.setdefault(sem, [sem, 0])
            token[1] = self.dma_cnt[sem]
        else:
            assert sem not in self.open
            token = [sem, self.dma_cnt[sem]]
        self.stream[eng].append((waits, fn, (sem, n)))
        self._commit(token, reads, writes)
        if out:
            self.out_tokens.append(token)

    def final_wait(self, eng):
        need = {}
        for sem, val in self.out_tokens:
            need[sem] = max(need.get(sem, 0), val)
        self.stream[eng].append((list(need.items()), None, None))

    def emit(self):
        nc = self.nc
        names = list(self.ENG) + sorted(self.dma_cnt.keys())
        with contextlib.ExitStack() as st:
            for nme in names:
                self.sems[nme] = st.enter_context(nc.semaphore("s_" + nme))
            block = st.enter_context(nc.Block())
            deco = {"pe": block.tensor, "act": block.scalar, "dve": block.vector,
                    "pool": block.gpsimd, "sp": block.sync}
            for e in self.ENG:
                self._emit_engine(deco[e], e)

    def _emit_engine(self, deco, e):
        stream = self.stream[e]
        sems = self.sems

        @deco
        def _(eng):
            for waits, fn, dmainfo in stream:
                for sem, val in waits:
                    eng.wait_ge(sems[sem], val)
                if fn is None:
                    continue
                r = fn(eng)
                if dmainfo is None:
                    r.then_inc(sems[e], 1)
                else:
                    sem, n = dmainfo
                    if not isinstance(r, (list, tuple)):
                        r = [r]
                    assert len(r) == n, (len(r), n)
                    for ins in r:
                        ins.then_inc(sems[sem], 16)


def t5_bucket_np(rel):
    half, max_exact = 16, 8
    rel = np.asarray(rel, np.int64)
    ret = np.where(rel > 0, half, 0)
    n = np.abs(rel)
    nf = np.maximum(n, 1).astype(np.float32)
    large = max_exact + (np.log(nf / np.float32(max_exact)) / np.float32(math.log(128 / max_exact))
                         * np.float32(half - max_exact)).astype(np.int32)
    large = np.minimum(large, half - 1)
    return ret + np.where(n < max_exact, n, large)


def t5_bucket_table(rel):
    return t5_bucket_np(rel)


def build_program(n_layers=DEPTH, tiles=None, do_setup_layer=True, stage=99):
    nc = bass.Bass("TRN2", target_bir_lowering=False)

    def din(name, shape):
        return nc.dram_tensor(name, list(shape), F32, kind="ExternalInput").ap()

    def dout(name, shape):
        return nc.dram_tensor(name, list(shape), F32, kind="ExternalOutput").ap()

    xp = din("xp", [2048, 1024]); xs = din("xs", [128, 1024])
    cbk = din("cbk", [4, 4, 128, 128]); cbv = din("cbv", [4, 4, 128, 128])
    cck = din("cck", [4, 4, 512, 256]); ccv = din("ccv", [4, 4, 512, 256])
    w_in = din("w_in", [4, 1024, 3072]); w_out = din("w_out", [4, 1024, 1024])
    g_pre = din("g_pre", [4, 1024]); g_post = din("g_post", [4, 1024])
    a_norm_g = din("a_norm_g", [4, 256]); a_ws = din("a_ws", [4, 4, 128, 128])
    a_bs = din("a_bs", [4, 4, 128]); b_sinks = din("b_sinks", [4, 8])
    c_rel = din("c_rel", [4, 257, 4]); t5 = din("t5", [32, 8])
    c_ident = din("c_ident", [128, 128]); c_oh = din("c_oh", [32, 255])
    c_maskT = din("c_maskT", [128, 128]); c_rep = din("c_rep", [32, 128])
    c_seq = din("c_seq", [4, 128]); c_mrow = din("c_mrow", [4, 512]); c_sel = din("c_sel", [4, 512])

    yp = dout("yp", [2048, 1024]); ys = dout("ys", [128, 1024])
    bkp = dout("bkp", [4, 128, 128]); bvp = dout("bvp", [4, 128, 128])
    ckp = dout("ckp", [4, 512, 256]); cvp = dout("cvp", [4, 512, 256])
    bks = dout("bks", [4, 128, 128]); bvs = dout("bvs", [4, 128, 128])
    cks = dout("cks", [4, 128, 256]); cvs = dout("cvs", [4, 128, 256])
    avs = dout("avs", [4, 128, 256])

    dbg_hy = dout("dbg_hy", [128, 1024]) if stage == 6 else None
    wbi = nc.dram_tensor("wbi", [DEPTH - 1, 1024, 3072], BF16, kind="Internal")
    wbo = nc.dram_tensor("wbo", [DEPTH - 1, 1024, 1024], BF16, kind="Internal")
    R1_B = nc.dram_tensor("R1_B", [8, 255], F32, kind="Internal")
    R1_C = [nc.dram_tensor(f"R1_C{l}", [4, 639], F32, kind="Internal") for l in range(DEPTH)]
    R_B = nc.dram_tensor("R_B", [8, 64, 255], F32, kind="Internal")
    R_C = [nc.dram_tensor(f"R_C{l}", [4, 64, 639], F32, kind="Internal") for l in range(DEPTH)]

    P = Plan(nc)
    st = contextlib.ExitStack()

    def sb(name, shape, dt=F32):
        return st.enter_context(nc.sbuf_tensor(name, list(shape), dt))

    with st:
        xres = sb("xres", [128, NT, 1024])
        Wi = sb("Wi", [128, 8, 3072], BF16)
        Wo = sb("Wo", [128, 8, 1024], BF16)
        gpost_b = sb("gpost_b", [128, 1024])
        gpre_b = sb("gpre_b", [128, 1024])
        gv_b = sb("gv_b", [128, 256])
        bs_t = sb("bs_t", [128, 16])
        bs_s = sb("bs_s", [128, 16])
        bs_st = sb("bs_st", [16, 128])
        sink_b = sb("sink_b", [128, 32])
        esink = sb("esink", [128, 32])
        t5_sb = sb("t5_sb", [32, 8])
        ident_f = sb("ident_f", [128, 128])
        identb = sb("identb", [128, 128], BF16)
        maskT = sb("maskT", [128, 128])
        rep_f = sb("rep_f", [32, 128])
        repb = sb("repb", [32, 128], BF16)
        seqb = sb("seqb", [4, 128], BF16)
        mrow4b = sb("mrow4b", [4, 512], BF16)
        selb = sb("selb", [4, 512], BF16)
        neghalf = sb("neghalf", [128, 1])
        scratch = sb("scratch", [128, 1280])
        Tq = scratch
        wss_st = sb("wss_st", [128, 4, 128])
        WsT = sb("WsT", [128, 4, 128], BF16)
        WsTs = sb("WsTs", [128, 4, 128], BF16)
        biasTB = [sb(f"biasTB{a}", [128, 2, 512], BF16) for a in range(2)]
        biasTC = [sb(f"biasTC{a}", [128, 5, 256], BF16) for a in range(2)]
        BkT = sb("BkT", [128, 2, 2, 128], BF16)
        CkT = sb("CkT", [128, 5, 2, 128], BF16)
        BV1 = sb("BV1", [128, 2, 2, 65], BF16)
        CV1 = sb("CV1", [128, 5, 4, 65], BF16)
        CV1c = CV1[:, 0:4, :, :]
        kcc = CkT[:, 0:4, :, :].rearrange("p a b c -> p a (b c)")
        CKK = [("CkT", i) for i in range(5)]
        CVK = [("CV1", i) for i in range(5)]
        BkTs = sb("BkTs", [128, 2, 128], BF16)
        CkTs = sb("CkTs", [128, 2, 128], BF16)
        BV1s = sb("BV1s", [128, 2, 65], BF16)
        CV1s = sb("CV1s", [128, 4, 65], BF16)
        kcb = sb("kcb", [128, 256], BF16)
        BkTc = sb("BkTc", [128, 4, 2, 128], BF16)
        BV1c = sb("BV1c", [128, 4, 2, 65], BF16)
        CkTc = sb("CkTc", [128, 2, 512], BF16)
        hy = sb("hy", [128, 1024], BF16)
        hyT = sb("hyT", [128, 8, 128], BF16)
        auav = sb("auav", [128, 512])
        gate = scratch[:, 0:1024]
        kdup = sb("kdup", [128, 256], BF16)
        kc_bf = sb("kc_bf", [128, 256], BF16)
        kvout = sb("kvout", [128, 768])
        hb = sb("hb", [128, 1024], BF16)
        hT = sb("hT", [128, 8, 128], BF16)
        qTB = [sb(f"qTB{i}", [128, 4, 128], BF16) for i in range(2)]
        qTC = [sb(f"qTC{i}", [128, 2, 128], BF16) for i in range(2)]
        vn_bf = sb("vn_bf", [128, 256], BF16)
        mixb = sb("mixb", [128, 256])
        E_C = kvout[0:4, 0:639]
        crelT = auav[0:4, 0:257]
        E_B = auav[0:8, 257:512]
        oh_sb = mixb[0:32, 0:255]
        vn_f = mixb
        PTbuf = sb("PTbuf", [128, 2560], BF16)
        OaccC = sb("OaccC", [128, 260])
        stat = sb("stat", [128, 32])
        den = sb("den", [128, 16])
        rden = sb("rden", [128, 12])

        PS = [st.enter_context(nc.psum_tensor(f"PS{i}", [128, 512], F32)) for i in range(8)]
        P0b = PS[0].bitcast(BF16)

        TQK = ["Tq", ("gate", 0), ("gate", 256), ("gate", 512), ("gate", 768)]

        def r3(ap, a):
            return ap.rearrange("p (a b) -> p a b", a=a)

        def ld(eng, sem, out_ap, in_ap, reads=(), writes=(), batch=False, **kw):
            P.dma(eng, sem, lambda q: [q.dma_start(out=out_ap, in_=in_ap, **kw)], 1,
                  reads=reads, writes=writes, batch=batch)

        def store(out_ap, in_ap, reads, sem="out", batch=False):
            P.dma("pool", sem, lambda q: [q.dma_start(out=out_ap, in_=in_ap)], 1,
                  reads=reads, out=True, batch=batch)

        def cp(eng, out_ap, in_ap, reads, writes):
            if eng == "act":
                P.op("act", lambda a: a.activation(out=out_ap, in_=in_ap, func=AF.Identity), reads, writes)
            else:
                P.op(eng, lambda v: v.tensor_copy(out=out_ap, in_=in_ap), reads, writes)

        def rstd_ops(ss_ap, tmp_ap, out_ap, n, eps, key):
            P.op("dve", lambda v: v.tensor_scalar(out=tmp_ap, in0=ss_ap, scalar1=1.0 / n, scalar2=eps,
                                                  op0=ALU.mult, op1=ALU.add),
                 reads=[key + "ss"], writes=[key + "ms"])
            P.op("pool", lambda g: g.tensor_tensor(out=out_ap, in0=tmp_ap, in1=neghalf[:], op=ALU.pow),
                 reads=[key + "ms", "neghalf"], writes=[key + "rs"])

        for t in range(16):
            ld("sp", "x", xres[:, t, :], xp[t * 128:(t + 1) * 128, :], writes=[("x", t)], batch=True)
        ld("sp", "x", xres[:, 16, :], xs, writes=[("x", 16)], batch=True)
        P.close_batch("x")
        ld("sp", "c", ident_f[:], c_ident, writes=["ident_f"], batch=True)
        ld("sp", "c", maskT[:], c_maskT, writes=["maskT"], batch=True)
        ld("sp", "c", rep_f[:], c_rep, writes=["rep_f"], batch=True)
        ld("sp", "c", oh_sb, c_oh, writes=["oh_sb", "mixb"], batch=True)
        ld("sp", "c", t5_sb[:], t5, writes=["t5_sb"], batch=True)
        ld("sp", "c", bs_st[:], a_bs.rearrange("l g t -> (l g) t"), writes=["bs_st"], batch=True)
        ld("sp", "c", sink_b[:], b_sinks.rearrange("l h -> (l h)").partition_broadcast(128),
           writes=["sink_b"], batch=True)
        P.close_batch("c")
        ld("pool", "cp", seqb[:], c_seq, writes=["seqb"], batch=True)
        ld("pool", "cp", mrow4b[:], c_mrow, writes=["mrow4b"], batch=True)
        ld("pool", "cp", selb[:], c_sel, writes=["selb"], batch=True)
        P.close_batch("cp")

        P.op("dve", lambda v: v.tensor_copy(out=identb[:], in_=ident_f[:]), ["ident_f"], ["identb"])
        P.op("dve", lambda v: v.tensor_copy(out=repb[:], in_=rep_f[:]), ["rep_f"], ["repb"])
        P.op("pool", lambda g: g.memset(neghalf[:], -0.5), [], ["neghalf"])
        P.op("pool", lambda g: g.memset(Tq[:], 0.0), [], TQK)
        P.op("pool", lambda g: g.memset(wss_st[:], 0.0), [], ["wss_st"])
        for i in range(2):
            P.op("pool", (lambda g, i=i: g.memset(qTB[i][:], 0.0)), [], ["qTB"])
            P.op("pool", (lambda g, i=i: g.memset(qTC[i][:], 0.0)), [], ["qTC"])
        for nme, tl in (("BV1", BV1), ("CV1", CV1), ("BV1s", BV1s), ("CV1s", CV1s),
                        ("BV1c", BV1c)):
            P.op("pool", (lambda g, tl=tl: g.memset(tl[:], 1.0)), [], [nme])
        P.op("act", lambda a: a.activation(out=esink[:], in_=sink_b[:], func=AF.Exp), ["sink_b"], ["esink"])

        P.op("pe", lambda pe: pe.transpose(out=PS[2][:, 0:16], in_=bs_st[:], identity=ident_f[0:16, 0:16]),
             ["bs_st", "ident_f"], ["PS2"])
        cp("dve", bs_t[:], PS[2][:, 0:16], ["PS2"], ["bs_t"])
        P.op("pe", lambda pe: pe.matmul(PS[1][:, 0:16], lhsT=rep_f[:], rhs=bs_t[0:32, :], start=True, stop=True),
             ["rep_f", "bs_t"], ["PS1"])
        cp("dve", bs_s[:], PS[1][:, 0:16], ["PS1"], ["bs_s"])

        def build_bias(E_ap, H, W, Rt, HP, L, NTL, dest, destkey, layer_tag, perm=False, ekeys=(), R1t=None):
            rkey = "R" + layer_tag
            P.dma("sp", "rw", lambda q: [q.dma_start(out=R1t.ap(), in_=E_ap)], 1,
                  reads=list(ekeys), writes=[rkey + "1"])
            P.dma("sp", "rw2", lambda q: [q.dma_start(
                out=Rt.ap(), in_=R1t.ap().unsqueeze(1).broadcast_to([H, 64, W]))], 1,
                reads=[rkey + "1"], writes=[rkey])
            width = 1024 // HP if HP == 4 else 640
            tqv = r3(Tq[:, 0:HP * width], HP)
            for A in range(2):
                off = 64 * A
                P.op("pool", lambda g: g.memset(Tq[:], NEG), [], TQK)

                def rd(q, off=off):
                    res = []
                    for hh in range(2):
                        src = bass.AP(Rt, hh * 64 * W + 63, [[W - 1, 64], [2 * 64 * W, HP], [1, L]])
                        res.append(q.dma_start(out=tqv[hh * 64:(hh + 1) * 64, :, off:off + L], in_=src))
                    return res
                P.dma("sp", "bias", rd, 2, reads=[rkey], writes=TQK)
                for j in range(NTL):
                    bank = 1 + (j % 2)

                    def tr(pe, j=j, bank=bank):
                        r = None
                        for hp in range(HP):
                            r = pe.transpose(out=PS[bank][:, hp * 128:(hp + 1) * 128],
                                             in_=tqv[:, hp, j * 128:(j + 1) * 128], identity=ident_f[:])
                        return r
                    P.op("pe", tr, TQK + ["ident_f"], [f"PS{bank}"])
                    if not perm:
                        cp("dve", dest[A][:, j, :], PS[bank][:, 0:HP * 128], [f"PS{bank}"], [destkey])
                    else:
                        for kh in range(2):
                            for hh in range(2):
                                o0 = kh * 256 + hh * 128
                                cp("dve", r3(dest[A][:, j, o0:o0 + 128], 2),
                                   r3(PS[bank][:, kh * 256:(kh + 1) * 256], 2)[:, :, hh * 64:(hh + 1) * 64],
                                   [f"PS{bank}"], [destkey])

        P.op("pe", lambda pe: pe.matmul(PS[3][0:8, 0:255], lhsT=t5_sb[:], rhs=oh_sb, start=True, stop=True),
             ["t5_sb", "oh_sb", "mixb"], ["PS3"])
        cp("dve", E_B, PS[3][0:8, 0:255], ["PS3"], ["EB", "auav"])
        build_bias(E_B, 8, 255, R_B, 4, 192, 2, biasTB, "biasTB", "B", perm=True, ekeys=["EB", "auav"], R1t=R1_B)

        def head_region(idx):
            if idx < 7:
                return 7, idx * 65
            return 3, (idx - 7) * 65

        wloaded = set()

        precast = []
        for ll in range(1, DEPTH):
            for kc in range(8):
                precast.append(("i", ll, kc))
            for kc in range(8):
                precast.append(("o", ll, kc))

        def issue_precast(n):
            for _ in range(n):
                if not precast:
                    break
                kind, ll, kc = precast.pop(0)
                if kind == "i":
                    P.dma("pool", "pc", (lambda q, ll=ll, kc=kc: [q.dma_start(
                        out=wbi.ap()[ll - 1, kc * 128:(kc + 1) * 128, :],
                        in_=w_in[ll, kc * 128:(kc + 1) * 128, :])]), 1, writes=[("wbi", ll)], batch=True)
                else:
                    P.dma("pool", "pc", (lambda q, ll=ll, kc=kc: [q.dma_start(
                        out=wbo.ap()[ll - 1, kc * 128:(kc + 1) * 128, :],
                        in_=w_out[ll, kc * 128:(kc + 1) * 128, :])]), 1, writes=[("wbo", ll)], batch=True)
            if not precast:
                P.close_batch("pc")

        def load_wi(l):
            if l == 0 or n_layers != DEPTH or tiles is not None:
                wsem = f"w{l % 2}"
                for kc in range(8):
                    P.dma("pool", wsem, (lambda q, kc=kc: [q.dma_start(
                        out=Wi[:, kc, :], in_=w_in[l, kc * 128:(kc + 1) * 128, :])]), 1, writes=["Wi"], batch=True)
                P.close_batch(wsem)
            else:
                wsem = f"wb{l % 2}"
                for kc in range(8):
                    P.dma("sp", wsem, (lambda q, kc=kc: [q.dma_start(
                        out=Wi[:, kc, :], in_=wbi.ap()[l - 1, kc * 128:(kc + 1) * 128, :])]), 1,
                        reads=[("wbi", l)], writes=["Wi"], batch=True)
                P.close_batch(wsem)

        def load_wo(l):
            if l == 0 or n_layers != DEPTH or tiles is not None:
                wsem = f"wo{l % 2}"
                for kc in range(8):
                    P.dma("pool", wsem, (lambda q, kc=kc: [q.dma_start(
                        out=Wo[:, kc, :], in_=w_out[l, kc * 128:(kc + 1) * 128, :])]), 1, writes=["Wo"], batch=True)
                P.close_batch(wsem)
            else:
                wsem = f"wbo{l % 2}"
                for kc in range(8):
                    P.dma("sp", wsem, (lambda q, kc=kc: [q.dma_start(
                        out=Wo[:, kc, :], in_=wbo.ap()[l - 1, kc * 128:(kc + 1) * 128, :])]), 1,
                        reads=[("wbo", l)], writes=["Wo"], batch=True)
                P.close_batch(wsem)

        def layer_setup(l):
            if l not in wloaded:
                load_wi(l)
                load_wo(l)
            psem = f"par{l % 2}"
            ws_st = r3(Tq[:, 0:512], 4)
            ld("sp", psem, gpost_b[:], g_post[l, :].partition_broadcast(128), writes=["gpost_b"], batch=True)
            ld("sp", psem, gpre_b[:], g_pre[l, :].partition_broadcast(128), writes=["gpre_b"], batch=True)
            ld("sp", psem, gv_b[:], a_norm_g[l, :].partition_broadcast(128), writes=["gv_b"], batch=True)
            for h in range(4):
                ld("sp", psem, crelT[h:h + 1, :], c_rel[l, :, h].unsqueeze(0), writes=["crelT", "auav"], batch=True,
                   allow_slow_non_contiguous=True)
            ld("sp", psem, ws_st, a_ws[l].rearrange("g t s -> t g s"), writes=TQK, batch=True)
            for j in range(4):
                ld("sp", psem, wss_st[32 * j:32 * j + 32, :, 32 * j:32 * j + 32],
                   a_ws[l, :, 0:32, 0:32].rearrange("g t s -> t g s"), writes=["wss_st"], batch=True)
            P.close_batch(psem)

            def trw(pe, src=ws_st):
                r = None
                for g in range(4):
                    r = pe.transpose(out=PS[1][:, g * 128:(g + 1) * 128], in_=src[:, g, :], identity=ident_f[:])
                return r
            P.op("pe", trw, TQK + ["ident_f"], ["PS1"])
            P.op("dve", lambda v: v.tensor_tensor(out=WsT[:], in0=r3(PS[1][:, :], 4),
                                                  in1=maskT[:].unsqueeze(1).broadcast_to([128, 4, 128]),
                                                  op=ALU.mult), ["PS1", "maskT"], ["WsT"])

            def trws(pe):
                r = None
                for g in range(4):
                    r = pe.transpose(out=PS[2][:, g * 128:(g + 1) * 128], in_=wss_st[:, g, :], identity=ident_f[:])
                return r
            P.op("pe", trws, ["wss_st", "ident_f"], ["PS2"])
            P.op("dve", lambda v: v.tensor_tensor(out=WsTs[:], in0=r3(PS[2][:, :], 4),
                                                  in1=maskT[:].unsqueeze(1).broadcast_to([128, 4, 128]),
                                                  op=ALU.mult), ["PS2", "maskT"], ["WsTs"])
            ECK = ["EC", "kvoutB", "kvoutC"]
            P.op("dve", lambda v: v.tensor_copy(out=E_C[:, 447:639], in_=crelT[:, 0:192]), ["crelT", "auav"], ECK)
            P.op("dve", lambda v: v.tensor_copy(out=E_C[:, 0:447], in_=crelT[:, 0:1].broadcast_to([4, 447])),
                 ["crelT", "auav"] + ECK, ECK)
            build_bias(E_C, 4, 639, R_C[l], 2, 576, 5, biasTC, "biasTC", "C", ekeys=ECK, R1t=R1_C[l])
            for s in range(4):
                P.dma("pool", "bv1c", (lambda q, s=s: [q.dma_start(
                    out=BV1c[:, s, :, 0:64], in_=cbv[l, s].rearrange("p (k d) -> p k d", k=2))]),
                    1, writes=["BV1c"], batch=True)
            P.close_batch("bv1c")
            for s in range(4):
                P.dma("pool", "kcb", (lambda q, s=s: [q.dma_start(
                    out=kcb[:].rearrange("p (k u d) -> p k u d", k=2, u=2),
                    in_=cbk[l, s].rearrange("p (k d) -> p k d", k=2).unsqueeze(2).broadcast_to([128, 2, 2, 64]))]),
                    1, writes=["kcb"])

                def trk(pe):
                    pe.transpose(out=P0b[:, 0:128], in_=kcb[:, 0:128], identity=identb[:])
                    return pe.transpose(out=P0b[:, 128:256], in_=kcb[:, 128:256], identity=identb[:])
                P.op("pe", trk, ["kcb", "identb"], ["PS0"])
                cp("dve", BkTc[:, s, :, :], r3(P0b[:, 0:256], 2), ["PS0"], ["BkTc"])

        prenormed = set()

        def prenorm_act(l, t):
            xt = xres[:, t, :]
            xk = ("x", t)
            P.op("act", lambda a: a.activation(out=hb[:], in_=xt, func=AF.Square, accum_out=stat[:, 0:1]),
                 [xk], ["hb", "n1ss"])
            rstd_ops(stat[:, 0:1], stat[:, 1:2], stat[:, 2:3], 1024, EPS, "n1")
            P.op("dve", lambda v: v.scalar_tensor_tensor(out=hb[:], in0=xt, scalar=stat[:, 2:3], in1=gpre_b[:],
                                                         op0=ALU.mult, op1=ALU.mult),
                 [xk, "n1rs", "gpre_b"], ["hb"])

        def prenorm_pe(l, t):
            def tr_h(pe):
                r = None
                for k in range(8):
                    r = pe.transpose(out=P0b[:, k * 128:(k + 1) * 128], in_=hb[:, k * 128:(k + 1) * 128],
                                     identity=identb[:])
                return r
            P.op("pe", tr_h, ["hb", "identb"], ["PS0"])
            cp("act", hT[:], r3(P0b[:, :], 8), ["PS0"], ["hT"])

        hoistedG = set()
        pending_tail = []

        def tm_group_g(bank, c0, c1, o0):
            def f(pe):
                r = None
                for k in range(8):
                    r = pe.matmul(PS[bank][:, o0:o0 + (c1 - c0)], lhsT=hT[:, k, :], rhs=Wi[:, k, c0:c1],
                                  start=(k == 0), stop=(k == 7))
                return r
            return f

        def g0_pe(l, t, bank):
            P.op("pe", tm_group_g(bank, 0, 512, 0), ["hT", "Wi"], [f"PS{bank}"])

        def g0_evac(l, t, bank):
            sample = (t == 16)
            cp("dve", auav[:], PS[bank][:, :], [f"PS{bank}"], ["auav"])
            P.op("act", lambda a: a.activation(out=vn_bf[:], in_=auav[:, 256:512], func=AF.Square,
                                               accum_out=stat[:, 3:4]), ["auav"], ["vn_bf", "n2ss"])
            rstd_ops(stat[:, 3:4], stat[:, 4:5], stat[:, 5:6], 256, EPS, "n2")
            if sample:
                P.op("dve", lambda v: v.scalar_tensor_tensor(
                    out=vn_f[:], in0=auav[:, 256:512], scalar=stat[:, 5:6], in1=gv_b[:],
                    op0=ALU.mult, op1=ALU.mult), ["auav", "n2rs", "gv_b"], ["mixb"])
                cp("dve", vn_bf[:], vn_f[:], ["mixb"], ["vn_bf"])
                store(avs[l], vn_f[:], ["mixb"], sem="o_av")
            else:
                P.op("dve", lambda v: v.scalar_tensor_tensor(
                    out=vn_bf[:], in0=auav[:, 256:512], scalar=stat[:, 5:6], in1=gv_b[:],
                    op0=ALU.mult, op1=ALU.mult), ["auav", "n2rs", "gv_b"], ["vn_bf"])

        def g4_pe(l, t, bank):
            P.op("pe", tm_group_g(bank, 2304, 2816, 0), ["hT", "Wi"], [f"PS{bank}"])

        def g4_evac(l, t, bank):
            sample = (t == 16)
            if sample:
                cv_dst, cvkey = CV1s[:, :, 0:64], "CV1s"
            else:
                cv_dst, cvkey = CV1[:, t % 5, :, 0:64], ("CV1", t % 5)
            cp("dve", kc_bf[:], PS[bank][:, 0:256], [f"PS{bank}"], ["kc_bf"])
            cp("dve", cv_dst, r3(PS[bank][:, 256:512], 4), [f"PS{bank}"], [cvkey])
            if sample or t >= 12:
                cp("dve", kvout[:, 256:768], PS[bank][:, :], [f"PS{bank}"], ["kvoutC"])

        def tile_ops(l, t):
            sample = (t == 16)
            last = (l == n_layers - 1)
            xt = xres[:, t, :]
            xk = ("x", t)
            out_bkv = sample or t == 15
            out_ckv = sample or t >= 12

            if (l, t) not in prenormed:
                prenorm_act(l, t)
                prenorm_pe(l, t)
            if stage < 2:
                return
            def tm_group(bank, c0, c1, o0):
                def f(pe):
                    r = None
                    for k in range(8):
                        r = pe.matmul(PS[bank][:, o0:o0 + (c1 - c0)], lhsT=hT[:, k, :], rhs=Wi[:, k, c0:c1],
                                      start=(k == 0), stop=(k == 7))
                    return r
                return f

            def fm_blocks(bank, c0, nb):
                def f(pe):
                    r = None
                    for j in range(nb):
                        for k in range(8):
                            r = pe.matmul(PS[bank][:, j * 128:(j + 1) * 128],
                                          lhsT=Wi[:, k, c0 + j * 128:c0 + (j + 1) * 128], rhs=hT[:, k, :],
                                          start=(k == 0), stop=(k == 7))
                    return r
                return f

            def gate_ops(bank, p0, width, g0):
                P.op("act", lambda a: a.activation(out=gate[:, g0:g0 + width], in_=PS[bank][:, p0:p0 + width],
                                                   func=AF.Tanh, scale=0.5),
                     [f"PS{bank}"], [("gate", g0)])
                P.op("dve", lambda v: v.scalar_tensor_tensor(
                    out=gate[:, g0:g0 + width], in0=gate[:, g0:g0 + width], scalar=1.0,
                    in1=PS[bank][:, p0:p0 + width], op0=ALU.add, op1=ALU.mult),
                    [f"PS{bank}", ("gate", g0)], [("gate", g0)])

            rin = ["hT", "Wi"]
            if (l, t) not in hoistedG:
                g0_pe(l, t, 1)
                g0_evac(l, t, 1)
            P.op("pe", tm_group(3, 1280, 1792, 0), rin, ["PS3"])
            slotB = t % 2
            slotC = t % 5
            if sample:
                bv_dst, bvkey = BV1s[:, :, 0:64], "BV1s"
                cv_dst, cvkey = CV1s[:, :, 0:64], "CV1s"
            else:
                bv_dst, bvkey = BV1[:, slotB, :, 0:64], ("BV1", slotB)
                cv_dst, cvkey = CV1[:, slotC, :, 0:64], ("CV1", slotC)
            P.op("act", lambda a: a.activation(out=gate[:, 256:512], in_=PS[3][:, 256:512], func=AF.Tanh, scale=0.5),
                 ["PS3"], [("gate", 256)])
            P.op("dve", lambda v: v.tensor_copy(
                out=kdup[:].rearrange("p (k u d) -> p k u d", k=2, u=2),
                in_=r3(PS[3][:, 0:128], 2).unsqueeze(2).broadcast_to([128, 2, 2, 64])),
                ["PS3", ("gate", 256)], ["kdup"])
            cp("dve", bv_dst, r3(PS[3][:, 128:256], 2), ["PS3"], [bvkey])
            P.op("dve", lambda v: v.scalar_tensor_tensor(
                out=gate[:, 256:512], in0=gate[:, 256:512], scalar=1.0, in1=PS[3][:, 256:512],
                op0=ALU.add, op1=ALU.mult), ["PS3", ("gate", 256)], [("gate", 256)])
            if out_bkv:
                cp("dve", kvout[:, 0:256], PS[3][:, 0:256], ["PS3"], ["kvoutB"])
            P.op("pe", tm_group(2, 512, 768, 0), rin, ["PS2"])
            P.op("pe", tm_group(2, 1792, 2048, 256), rin, ["PS2"])
            g13 = r3(gate[:, 0:1024], 2)[:, :, 0:256]
            P.op("act", lambda a: a.activation(out=g13, in_=r3(PS[2][:, :], 2), func=AF.Tanh, scale=0.5),
                 ["PS2"], [("gate", 0), ("gate", 512)])
            P.op("dve", lambda v: v.scalar_tensor_tensor(out=g13, in0=g13, scalar=1.0, in1=r3(PS[2][:, :], 2),
                                                         op0=ALU.add, op1=ALU.mult),
                 ["PS2", ("gate", 0), ("gate", 512)], [("gate", 0), ("gate", 512)])
            if (l, t) not in hoistedG:
                g4_pe(l, t, 1)
                g4_evac(l, t, 1)
            P.op("pe", tm_group(1, 2816, 3072, 0), rin, ["PS1"])
            gate_ops(1, 0, 256, 768)
            while pending_tail:
                pending_tail.pop(0)()
            def tr_k(pe):
                pe.transpose(out=P0b[:, 0:128], in_=kdup[:, 0:128], identity=identb[:])
                pe.transpose(out=P0b[:, 128:256], in_=kdup[:, 128:256], identity=identb[:])
                pe.transpose(out=P0b[:, 256:384], in_=kc_bf[:, 0:128], identity=identb[:])
                return pe.transpose(out=P0b[:, 384:512], in_=kc_bf[:, 128:256], identity=identb[:])
            P.op("pe", tr_k, ["kdup", "kc_bf", "identb"], ["PS0"])
            if sample:
                cp("dve", BkTs[:], r3(P0b[:, 0:256], 2), ["PS0"], ["BkTs"])
                cp("dve", CkTs[:], r3(P0b[:, 256:512], 2), ["PS0"], ["CkTs"])
            else:
                cp("dve", BkT[:, slotB, :, :], r3(P0b[:, 0:256], 2), ["PS0"], [("BkT", slotB)])
                cp("dve", CkT[:, slotC, :, :], r3(P0b[:, 256:512], 2), ["PS0"], [("CkT", slotC)])

            P.op("pe", fm_blocks(4, 768, 4), rin, ["PS4"])
            for i in range(2):
                P.op("act", (lambda a, i=i: a.activation(out=qTB[i][i * 64:(i + 1) * 64], in_=r3(PS[4][i * 64:(i + 1) * 64, :], 4),
                                                        func=AF.Identity, scale=0.125)), ["PS4"], ["qTB"])
            P.op("pe", fm_blocks(3, 2048, 2), rin, ["PS3"])
            for i in range(2):
                P.op("act", (lambda a, i=i: a.activation(out=qTC[i][i * 64:(i + 1) * 64], in_=r3(PS[3][i * 64:(i + 1) * 64, 0:256], 2),
                                                        func=AF.Identity, scale=0.125)), ["PS3"], ["qTC"])

            if stage < 3:
                return
            last_tile = (t == (NT - 1 if tiles is None else list(tiles)[-1]))
            hoist = t + 1 < NT and (tiles is None or (t + 1) in tiles)
            if hoist:
                prenormed.add((l, t + 1))
                prenorm_act(l, t + 1)
            if l == 0 and n_layers == DEPTH and tiles is None:
                issue_precast(4 if t < 11 else 48)
            def kvstore(o, i, r):
                store(o, i, r, sem=f"o_kv{t % 2}", batch=True)
            if out_bkv:
                if sample:
                    kvstore(bks[l], kvout[:, 0:128], ["kvoutB"])
                    kvstore(bvs[l], kvout[:, 128:256], ["kvoutB"])
                else:
                    kvstore(bkp[l], kvout[:, 0:128], ["kvoutB"])
                    kvstore(bvp[l], kvout[:, 128:256], ["kvoutB"])
            if out_ckv:
                if sample:
                    kvstore(cks[l], kvout[:, 256:512], ["kvoutC"])
                    kvstore(cvs[l], kvout[:, 512:768], ["kvoutC"])
                else:
                    r0 = (t - 12) * 128
                    kvstore(ckp[l, r0:r0 + 128, :], kvout[:, 256:512], ["kvoutC"])
                    kvstore(cvp[l, r0:r0 + 128, :], kvout[:, 512:768], ["kvoutC"])
            P.close_batch(f"o_kv{t % 2}")

            if stage < 4:
                return
            SUB = 99
            if stage == 4 and SUB < 1:
                return
            wst, wkey = (WsTs, "WsTs") if sample else (WsT, "WsT")
            bsx = bs_s if sample else bs_t

            def amix(pe):
                r = None
                for g in range(4):
                    r = pe.matmul(PS[4][:, g * 64:(g + 1) * 64], lhsT=wst[:, g, :], rhs=vn_bf[:, g * 64:(g + 1) * 64],
                                  start=True, stop=True)
                return r
            P.op("pe", amix, [wkey, "vn_bf"], ["PS4"])
            if stage == 4 and SUB < 2:
                return
            P.op("dve", lambda v: v.tensor_tensor(
                out=r3(mixb[:], 4), in0=r3(PS[4][:, 0:256], 4),
                in1=bsx[:, l * 4:(l + 1) * 4].unsqueeze(2).broadcast_to([128, 4, 64]), op=ALU.add),
                ["PS4", "bs_t", "bs_s"], ["mixb"])
            P.op("dve", lambda v: v.tensor_tensor(out=mixb[:], in0=mixb[:], in1=auav[:, 0:256], op=ALU.mult),
                 ["mixb", "auav"], ["mixb"])
            P.op("dve", lambda v: v.tensor_tensor(out=hy[:, 0:256], in0=mixb[:], in1=gate[:, 0:256], op=ALU.mult),
                 ["mixb", ("gate", 0)], ["hy"])

            if stage < 5:
                return
            stctr = [0]

            def emit_st_exp(g, slot):
                base = slot * 1280
                nc_ = g["ncols"]
                for i, kt in enumerate(g["kts"]):
                    bank = (5, 6, 1, 2)[stctr[0] % 4]
                    stctr[0] += 1
                    p0, plen = kt["p0"], kt["plen"]
                    P.op("pe", (lambda pe, kt=kt, bank=bank: kt["st"](pe, bank)), kt["reads"], [f"PS{bank}"])
                    P.op("act", (lambda a, bank=bank, p0=p0, plen=plen, c0=base + i * nc_:
                                 a.activation(out=PTbuf[p0:p0 + plen, c0:c0 + nc_],
                                              in_=PS[bank][p0:p0 + plen, 0:nc_], func=AF.Exp)),
                         [f"PS{bank}"], [("PT", slot)])

            def emit_pv(g, slot):
                base = slot * 1280
                nc_ = g["ncols"]
                w = g["width"]
                kts = g["kts"]
                n = len(kts)

                def pv(pe):
                    r = None
                    for hl, (hidx, cb) in enumerate(g["heads"]):
                        ob, oc = head_region(hidx)
                        if sample:
                            oap = PS[ob][:, oc:oc + 65]
                        else:
                            oap = PS[ob][g["hf"] * 64:(g["hf"] + 1) * 64, oc:oc + 65]
                        for i, kt in enumerate(kts):
                            p0, plen = kt["p0"], kt["plen"]
                            c0 = base + i * nc_ + cb * w
                            r = pe.matmul(oap, lhsT=PTbuf[p0:p0 + plen, c0:c0 + w], rhs=kt["vrhs"](hl),
                                          start=(i == 0), stop=(i == n - 1))
                    return r
                vreads = sorted({k for kt in kts for k in kt["vreads"]}, key=str)
                obanks = sorted({f"PS{head_region(h)[0]}" for h, _ in g["heads"]})
                P.op("pe", pv, [("PT", slot)] + vreads, obanks)
                if g.get("acc") is not None:
                    first = g["acc"]
                    if first:
                        P.op("dve", lambda v: v.tensor_copy(out=OaccC[:], in_=PS[3][:, 65:325]), ["PS3"], ["OaccC"])
                    else:
                        P.op("dve", lambda v: v.tensor_tensor(out=OaccC[:], in0=OaccC[:], in1=PS[3][:, 65:325],
                                                              op=ALU.add), ["PS3", "OaccC"], ["OaccC"])

            def run_groups(groups, nslots):
                prev = None
                for gi, g in enumerate(groups):
                    slot = gi % nslots
                    emit_st_exp(g, slot)
                    if prev is not None and nslots > 1:
                        emit_pv(*prev)
                        prev = None
                    if nslots == 1:
                        emit_pv(g, slot)
                    else:
                        prev = (g, slot)
                if prev is not None:
                    emit_pv(*prev)

            groups = []
            if not sample:
                for hf in range(2):
                    c = 2 * t + hf
                    q0 = hf * 64
                    for kind in "BC":
                        n_prev = 2 if kind == "B" else 8
                        ts = (c - n_prev) // 2
                        ktiles = {}
                        for cc in range(c - n_prev, c + 1):
                            if cc >= 0:
                                ktiles.setdefault(cc // 2, []).append(cc % 2)
                        ktl = []
                        for tt in sorted(ktiles):
                            ktl.append((tt, 0, 128, tt - ts))
                        if kind == "B":
                            for kh in range(2):
                                kts = []
                                for tt, p0, plen, j in ktl:
                                    sl = tt % 2

                                    def st_fn(pe, bank, j=j, kh=kh, p0=p0, plen=plen, hf=hf, sl=sl, q0=q0):
                                        DST = "bq"
                                        r = None
                                        if "b" in DST:
                                            r = pe.matmul(PS[bank][p0:p0 + plen, 0:256],
                                                          lhsT=identb[p0:p0 + plen, p0:p0 + plen],
                                                          rhs=biasTB[hf][p0:p0 + plen, j, kh * 256:(kh + 1) * 256],
                                                          start=True, stop=("q" not in DST))
                                        for hh in (range(2) if "q" in DST else []):
                                            r = pe.matmul(PS[bank][p0:p0 + plen, hh * 128:(hh + 1) * 128],
                                                          lhsT=BkT[:, sl, kh, p0:p0 + plen],
                                                          rhs=qTB[hh][:, 2 * kh:2 * kh + 2, q0:q0 + 64],
                                                          start=("b" not in DST and hh == 0), stop=(hh == 1))
                                        return r
                                    kts.append(dict(p0=p0, plen=plen, st=st_fn,
                                                    reads=["biasTB", "identb", ("BkT", sl), "qTB"],
                                                    vreads=[("BV1", sl)],
                                                    vrhs=(lambda hl, p0=p0, plen=plen, sl=sl, kh=kh:
                                                          BV1[p0:p0 + plen, sl, kh, 0:65])))
                                groups.append(dict(kts=kts, ncols=256, width=64, hf=hf,
                                                   heads=[(4 * kh + hg, (hg % 2) * 2 + hg // 2) for hg in range(4)]))
                        else:
                            kts = []
                            for tt, p0, plen, j in ktl:
                                sl = tt % 5

                                def st_fn(pe, bank, j=j, p0=p0, plen=plen, hf=hf, sl=sl, q0=q0):
                                    pe.matmul(PS[bank][p0:p0 + plen, 0:256],
                                              lhsT=identb[p0:p0 + plen, p0:p0 + plen],
                                              rhs=biasTC[hf][p0:p0 + plen, j, :], start=True, stop=False)
                                    r = None
                                    for h in range(4):
                                        b0 = (h % 2) * 64
                                        r = pe.matmul(PS[bank][p0:p0 + plen, h * 64:(h + 1) * 64],
                                                      lhsT=CkT[:, sl, h // 2, p0:p0 + plen],
                                                      rhs=qTC[h % 2][:, h // 2, q0:q0 + 64],
                                                      start=False, stop=(h == 3))
                                    return r
                                kts.append(dict(p0=p0, plen=plen, st=st_fn,
                                                reads=["biasTC", "identb", ("CkT", sl), "qTC"],
                                                vreads=[("CV1", sl)],
                                                vrhs=(lambda hl, p0=p0, plen=plen, sl=sl:
                                                      CV1[p0:p0 + plen, sl, hl, 0:65])))
                            groups.append(dict(kts=kts, ncols=256, width=64, hf=hf,
                                               heads=[(8 + h, h) for h in range(4)]))
                run_groups(groups, 2)
            else:
                for kh in range(2):
                    kts = []
                    for s in range(4):
                        def st_fn(pe, bank, kh=kh, s=s):
                            rb = r3(biasTB[0][:, 0, kh * 256:(kh + 1) * 256], 4)[:, :, 0:32]
                            pe.matmul(PS[bank][:, :], lhsT=identb[:],
                                      rhs=rb.unsqueeze(2).broadcast_to([128, 4, 4, 32]), start=True, stop=False)
                            pe.matmul(PS[bank][:, :], lhsT=selb[:, s * 128:(s + 1) * 128], rhs=mrow4b[:],
                                      start=False, stop=False)
                            r = None
                            for hh in range(2):
                                r = pe.matmul(PS[bank][:, hh * 256:(hh + 1) * 256],
                                              lhsT=BkTc[:, s, kh, :],
                                              rhs=qTB[hh][:, 2 * kh:2 * kh + 2, :],
                                              start=False, stop=(hh == 1))
                            return r
                        kts.append(dict(p0=0, plen=128, st=st_fn,
                                        reads=["biasTB", "identb", "selb", "mrow4b", "BkTc", "qTB"],
                                        vreads=["BV1c"], vrhs=(lambda hl, s=s, kh=kh: BV1c[:, s, kh, 0:65])))

                    def st_fn(pe, bank, kh=kh):
                        rb = r3(biasTB[0][0:32, 1, kh * 256:(kh + 1) * 256], 4)[:, :, 0:32]
                        pe.matmul(PS[bank][:, :], lhsT=repb[:],
                                  rhs=rb.unsqueeze(2).broadcast_to([32, 4, 4, 32]), start=True, stop=False)
                        pe.matmul(PS[bank][:, :], lhsT=seqb[:], rhs=mrow4b[:], start=False, stop=False)
                        r = None
                        for hh in range(2):
                            r = pe.matmul(PS[bank][:, hh * 256:(hh + 1) * 256],
                                          lhsT=BkTs[:, kh, :],
                                          rhs=qTB[hh][:, 2 * kh:2 * kh + 2, :],
                                          start=False, stop=(hh == 1))
                        return r
                    kts.append(dict(p0=0, plen=128, st=st_fn,
                                    reads=["biasTB", "repb", "seqb", "mrow4b", "BkTs", "qTB"],
                                    vreads=["BV1s"], vrhs=(lambda hl, kh=kh: BV1s[:, kh, 0:65])))
                    run_groups([dict(kts=kts, ncols=512, width=128, hf=0,
                                     heads=[(4 * kh + hg, (hg % 2) * 2 + hg // 2) for hg in range(4)])], 1)
                for s in range(4):
                    P.dma("pool", "kcc", (lambda q, s=s: [q.dma_start(
                        out=kcc, in_=cck[l, s].rearrange("(kt p) c -> p kt c", p=128))]), 1, writes=CKK)
                    P.dma("pool", "cv1c", (lambda q, s=s: [q.dma_start(
                        out=CV1c[:, kt, :, 0:64],
                        in_=ccv[l, s, kt * 128:(kt + 1) * 128, :].rearrange("p (h d) -> p h d", h=4))
                        for kt in range(4)]), 4, writes=CVK)

                    def trc(pe):
                        r = None
                        for blk in range(2):
                            for kt in range(4):
                                o0 = (blk * 4 + kt) * 128
                                r = pe.transpose(out=P0b[:, o0:o0 + 128], in_=kcc[:, kt, blk * 128:(blk + 1) * 128],
                                                 identity=identb[:])
                        return r
                    P.op("pe", trc, CKK + ["identb"], ["PS0"])
                    cp("dve", CkTc[:], r3(P0b[:, :], 2), ["PS0"], ["CkTc"])
                    kts = []
                    for kt in range(4):
                        def st_fn(pe, bank, kt=kt, s=s):
                            rb = r3(biasTC[0][:, kt, :], 4)[:, :, 0:32]
                            pe.matmul(PS[bank][:, :], lhsT=identb[:],
                                      rhs=rb.unsqueeze(2).broadcast_to([128, 4, 4, 32]), start=True, stop=False)
                            pe.matmul(PS[bank][:, :], lhsT=selb[:, s * 128:(s + 1) * 128], rhs=mrow4b[:],
                                      start=False, stop=False)
                            r = None
                            for h in range(4):
                                b0 = (h % 2) * 64
                                r = pe.matmul(PS[bank][:, h * 128:(h + 1) * 128],
                                              lhsT=CkTc[:, h // 2, kt * 128:(kt + 1) * 128],
                                              rhs=qTC[h % 2][:, h // 2, :], start=False, stop=(h == 3))
                            return r
                        kts.append(dict(p0=0, plen=128, st=st_fn,
                                        reads=["biasTC", "identb", "selb", "mrow4b", "CkTc", "qTC"],
                                        vreads=CVK, vrhs=(lambda hl, kt=kt: CV1c[:, kt, hl, 0:65])))
                    run_groups([dict(kts=kts, ncols=512, width=128, hf=0, acc=(s == 0),
                                     heads=[(8 + h, h) for h in range(4)])], 1)

                def st_fn(pe, bank):
                    rb = r3(biasTC[0][0:32, 4, :], 4)[:, :, 0:32]
                    pe.matmul(PS[bank][:, :], lhsT=repb[:],
                              rhs=rb.unsqueeze(2).broadcast_to([32, 4, 4, 32]), start=True, stop=False)
                    pe.matmul(PS[bank][:, :], lhsT=seqb[:], rhs=mrow4b[:], start=False, stop=False)
                    r = None
                    for h in range(4):
                        b0 = (h % 2) * 64
                        r = pe.matmul(PS[bank][:, h * 128:(h + 1) * 128], lhsT=CkTs[:, h // 2, :],
                                      rhs=qTC[h % 2][:, h // 2, :], start=False, stop=(h == 3))
                    return r
                run_groups([dict(kts=[dict(p0=0, plen=128, st=st_fn,
                                           reads=["biasTC", "repb", "seqb", "mrow4b", "CkTs", "qTC"],
                                           vreads=["CV1s"], vrhs=(lambda hl: CV1s[:, hl, 0:65]))],
                                 ncols=512, width=128, hf=0, acc=False,
                                 heads=[(8 + h, h) for h in range(4)])], 1)

            if stage < 6:
                return
            if last_tile and l + 1 < n_layers:
                wloaded.add(l + 1)
                load_wi(l + 1)
            if hoist:
                prenorm_pe(l, t + 1)
                hoistedG.add((l, t + 1))
                g0_pe(l, t + 1, 1)
                g4_pe(l, t + 1, 2)
            P.op("dve", lambda v: v.tensor_tensor(
                out=den[:, 0:7].unsqueeze(2), in0=r3(PS[7][:, 0:455], 7)[:, :, 64:65],
                in1=esink[:, l * 8:l * 8 + 7].unsqueeze(2), op=ALU.add),
                ["PS7", "esink"], ["den"])
            P.op("dve", lambda v: v.tensor_tensor(
                out=den[:, 7:8], in0=PS[3][:, 64:65], in1=esink[:, l * 8 + 7:l * 8 + 8], op=ALU.add),
                ["PS3", "esink", "den"], ["den"])
            if sample:
                P.op("dve", lambda v: v.tensor_copy(out=den[:, 8:12].unsqueeze(2), in_=r3(OaccC[:], 4)[:, :, 64:65]),
                     ["OaccC", "den"], ["den"])
            else:
                P.op("dve", lambda v: v.tensor_copy(out=den[:, 8:12].unsqueeze(2),
                                                    in_=r3(PS[3][:, 65:325], 4)[:, :, 64:65]),
                     ["PS3", "den"], ["den"])
            P.op("dve", lambda v: v.reciprocal(out=rden[:], in_=den[:, 0:12]), ["den"], ["rden"])
            gk = [("gate", 256), ("gate", 512), ("gate", 768)]
            P.op("dve", lambda v: v.tensor_tensor(out=r3(gate[:, 256:1024], 12), in0=r3(gate[:, 256:1024], 12),
                                                  in1=rden[:].unsqueeze(2).broadcast_to([128, 12, 64]), op=ALU.mult),
                 ["rden"] + gk, gk)
            P.op("dve", lambda v: v.tensor_tensor(out=r3(hy[:, 256:704], 7), in0=r3(PS[7][:, 0:455], 7)[:, :, 0:64],
                                                  in1=r3(gate[:, 256:704], 7), op=ALU.mult), ["PS7"] + gk, ["hy"])
            if sample:
                P.op("dve", lambda v: v.tensor_tensor(out=hy[:, 704:768], in0=PS[3][:, 0:64], in1=gate[:, 704:768],
                                                      op=ALU.mult), ["PS3"] + gk, ["hy"])
                P.op("dve", lambda v: v.tensor_tensor(out=r3(hy[:, 768:1024], 4), in0=r3(OaccC[:], 4)[:, :, 0:64],
                                                      in1=r3(gate[:, 768:1024], 4), op=ALU.mult), ["OaccC"] + gk, ["hy"])
            else:
                P.op("dve", lambda v: v.tensor_tensor(out=r3(hy[:, 704:1024], 5), in0=r3(PS[3][:, 0:325], 5)[:, :, 0:64],
                                                      in1=r3(gate[:, 704:1024], 5), op=ALU.mult), ["PS3"] + gk, ["hy"])

            if stage < 7:
                if stage == 6:
                    store(dbg_hy, hy[:], ["hy"])
                return
            def tr_y(pe):
                r = None
                for k in range(8):
                    r = pe.transpose(out=P0b[:, k * 128:(k + 1) * 128], in_=hy[:, k * 128:(k + 1) * 128],
                                     identity=identb[:])
                return r
            P.op("pe", tr_y, ["hy", "identb"], ["PS0"])
            cp("dve", hyT[:], r3(P0b[:, :], 8), ["PS0"], ["hyT"])
            if hoist:
                g0_evac(l, t + 1, 1)
                g4_evac(l, t + 1, 2)
            for n in range(2):
                def oproj(pe, n=n):
                    r = None
                    for k in range(8):
                        r = pe.matmul(PS[5 + n][:, :], lhsT=hyT[:, k, :], rhs=Wo[:, k, n * 512:(n + 1) * 512],
                                      start=(k == 0), stop=(k == 7))
                    return r
                P.op("pe", oproj, ["hyT", "Wo"], [f"PS{5 + n}"])
            if last_tile and l + 1 < n_layers:
                load_wo(l + 1)
            P.op("act", lambda a: a.activation(out=hy[:, 0:512], in_=PS[5][:, :], func=AF.Square,
                                               accum_out=stat[:, 16:17]), ["PS5"], ["hy", "n3a"])
            P.op("act", lambda a: a.activation(out=hy[:, 512:1024], in_=PS[6][:, :], func=AF.Square,
                                               accum_out=stat[:, 17:18]), ["PS6"], ["hy", "n3b"])
            def tail():
                P.op("dve", lambda v: v.tensor_tensor(out=stat[:, 18:19], in0=stat[:, 16:17], in1=stat[:, 17:18],
                                                      op=ALU.add), ["n3a", "n3b"], ["n3ss"])
                rstd_ops(stat[:, 18:19], stat[:, 19:20], stat[:, 20:21], 1024, 4.0 * EPS, "n3")
                for n in range(2):
                    P.op("dve", (lambda v, n=n: v.scalar_tensor_tensor(
                        out=PS[5 + n][:, :], in0=PS[5 + n][:, :], scalar=stat[:, 20:21],
                        in1=gpost_b[:, n * 512:(n + 1) * 512], op0=ALU.mult, op1=ALU.mult)),
                        [f"PS{5 + n}", "n3rs", "gpost_b"], [f"PS{5 + n}"])
                    P.op("dve", (lambda v, n=n: v.tensor_tensor(
                        out=xt[:, n * 512:(n + 1) * 512], in0=xt[:, n * 512:(n + 1) * 512], in1=PS[5 + n][:, :],
                        op=ALU.add)), [f"PS{5 + n}", xk], [xk])
                if last:
                    if sample:
                        store(ys, xt, [xk])
                    else:
                        store(yp[t * 128:(t + 1) * 128, :], xt, [xk])
            if last_tile:
                tail()
            else:
                pending_tail.append(tail)

        for l in range(n_layers):
            if do_setup_layer:
                layer_setup(l)
            for t in (range(NT) if tiles is None else tiles):
                tile_ops(l, t)
        P.final_wait("pool")
        P.emit()
    return nc


_CACHE = {}


def _consts():
    ident = np.eye(128, dtype=np.float32)
    rel = np.arange(255) - 191
    bucket = t5_bucket_table(rel)
    oh = np.zeros((32, 255), np.float32)
    oh[bucket, np.arange(255)] = 1.0
    s_idx = np.arange(128)[:, None]
    t_idx = np.arange(128)[None, :]
    maskT = (t_idx >= s_idx).astype(np.float32)
    rep = np.zeros((32, 128), np.float32)
    rep[np.arange(128) % 32, np.arange(128)] = 1.0
    seq = np.zeros((4, 128), np.float32)
    seq[np.arange(128) // 32, np.arange(128)] = 1.0
    mrow = np.full((4, 4, 4, 32), NEG, np.float32)
    for s in range(4):
        mrow[s, :, s, :] = 0.0
    sel = np.zeros((4, 4, 128), np.float32)
    for s in range(4):
        sel[s, s, :] = 1.0
    return dict(c_ident=ident, c_oh=oh, c_maskT=maskT, c_rep=rep, c_seq=seq,
                c_mrow=np.ascontiguousarray(mrow.reshape(4, 512)),
                c_sel=np.ascontiguousarray(sel.reshape(4, 512)))


def kernel(x_prompt, x_sample, cache_b_k, cache_b_v, cache_c_k, cache_c_v,
           g_pre, g_post, w_in, w_out, a_norm_g, a_ws, a_bs, b_sinks, c_rel_bias, t5_bias):
    f = lambda a: np.ascontiguousarray(np.asarray(a, dtype=np.float32))
    if "nc" not in _CACHE:
        _CACHE["nc"] = build_program()
    nc = _CACHE["nc"]
    consts = _consts()
    shared = dict(w_in=f(w_in), w_out=f(w_out), g_pre=f(g_pre), g_post=f(g_post), a_norm_g=f(a_norm_g),
                  a_ws=f(a_ws), a_bs=f(a_bs), b_sinks=f(b_sinks), c_rel=f(c_rel_bias), t5=f(t5_bias), **consts)
    x_prompt = f(x_prompt); x_sample = f(x_sample)
    cbk = f(cache_b_k).reshape(4, 32, 128, 128); cbv = f(cache_b_v).reshape(4, 32, 128, 128)
    cck = f(cache_c_k).reshape(4, 32, 512, 256); ccv = f(cache_c_v).reshape(4, 32, 512, 256)
    in_maps = []
    for i in range(8):
        m = dict(shared)
        m["xp"] = x_prompt[i]
        m["xs"] = np.ascontiguousarray(x_sample[4 * i:4 * i + 4].reshape(128, 1024))
        m["cbk"] = np.ascontiguousarray(cbk[:, 4 * i:4 * i + 4])
        m["cbv"] = np.ascontiguousarray(cbv[:, 4 * i:4 * i + 4])
        m["cck"] = np.ascontiguousarray(cck[:, 4 * i:4 * i + 4])
        m["ccv"] = np.ascontiguousarray(ccv[:, 4 * i:4 * i + 4])
        in_maps.append(m)
    res = run_bass_kernel_spmd(nc, in_maps, core_ids=list(range(8)))
    R = res.results
    y_prompt = np.stack([np.asarray(r["yp"]) for r in R], 0)
    y_sample = np.concatenate([np.asarray(r["ys"]).reshape(4, 32, 1024) for r in R], 0)
    def prm(k, rows, h):
        return np.stack([np.asarray(r[k]).reshape(4, rows, h, 64) for r in R], 1)
    def smp(k, h):
        return np.concatenate([np.asarray(r[k]).reshape(4, 4, 32, h, 64) for r in R], 1)
    a_v = np.concatenate([np.asarray(r["avs"]).reshape(4, 4, 32, 256) for r in R], 1)
    return (y_prompt.astype(np.float32), y_sample.astype(np.float32),
            prm("bkp", 128, 2), prm("bvp", 128, 2), prm("ckp", 512, 4), prm("cvp", 512, 4),
            smp("bks", 2), smp("bvs", 2), smp("cks", 4), smp("cvs", 4), a_v)
```

```python
import contextlib
import math
import os

import numpy as np

import concourse.bass as bass
import concourse.mybir as mybir
from concourse.alu_op_type import AluOpType as ALU
from concourse.bass_utils import run_bass_kernel_spmd

F32 = mybir.dt.float32
BF16 = mybir.dt.bfloat16
AF = mybir.ActivationFunctionType

DEPTH = 4
NT = 17
EPS = 1e-6
NEG = -30000.0


class Plan:
    ENG = ["pe", "act", "dve", "pool", "sp"]

    def __init__(self, nc):
        self.nc = nc
        self.stream = {e: [] for e in self.ENG}
        self.cnt = {e: 0 for e in self.ENG}
        self.waited = {e: {} for e in self.ENG}
        self.last_w = {}
        self.readers = {}
        self.dma_cnt = {}
        self.sems = {}
        self.out_tokens = []
        self.open = {}

    def _deps(self, eng, reads, writes, skip_sem=None, is_dma=False):
        deps = []
        for k in reads:
            t = self.last_w.get(k)
            if t is not None:
                deps.append((t, "raw"))
        for k in writes:
            t = self.last_w.get(k)
            if t is not None:
                deps.append((t, "waw"))
            for t in self.readers.get(k, ()):
                deps.append((t, "war"))
        need = {}
        for tok, kind in deps:
            if skip_sem is not None and tok[0] == skip_sem:
                continue
            assert not any(tok is o for o in self.open.values()), ("dependency on open DMA batch", tok)
            sem, val = tok
            if sem == eng and not is_dma and eng == "pe":
                continue
            need[sem] = max(need.get(sem, 0), val)
        waits = []
        for sem, val in need.items():
            if self.waited[eng].get(sem, 0) >= val:
                continue
            self.waited[eng][sem] = val
            waits.append((sem, val))
        return waits

    def _commit(self, token, reads, writes):
        for k in writes:
            self.last_w[k] = token
            self.readers[k] = []
        for k in reads:
            if k in writes:
                continue
            self.readers.setdefault(k, []).append(token)

    def op(self, eng, fn, reads=(), writes=()):
        reads, writes = tuple(reads), tuple(writes)
        waits = self._deps(eng, reads, writes)
        self.cnt[eng] += 1
        token = [eng, self.cnt[eng]]
        self.stream[eng].append((waits, fn, None))
        self._commit(token, reads, writes)

    def close_batch(self, sem):
        self.open.pop(sem, None)

    def dma(self, eng, sem, fn, n, reads=(), writes=(), out=False, batch=False):
        reads, writes = tuple(reads), tuple(writes)
        waits = self._deps(eng, reads, writes, skip_sem=sem, is_dma=True)
        self.dma_cnt[sem] = self.dma_cnt.get(sem, 0) + 16 * n
        if batch:
            token = self.open.setdefault(sem, [sem, 0])
            token[1] = self.dma_cnt[sem]
        else:
            assert sem not in self.open
            token = [sem, self.dma_cnt[sem]]
        self.stream[eng].append((waits, fn, (sem, n)))
        self._commit(token, reads, writes)
        if out:
            self.out_tokens.append(token)

    def final_wait(self, eng):
        need = {}
        for sem, val in self.out_tokens:
            need[sem] = max(need.get(sem, 0), val)
        self.stream[eng].append((list(need.items()), None, None))

    def emit(self):
        nc = self.nc
        names = list(self.ENG) + sorted(self.dma_cnt.keys())
        with contextlib.ExitStack() as st:
            for nme in names:
                self.sems[nme] = st.enter_context(nc.semaphore("s_" + nme))
            block = st.enter_context(nc.Block())
            deco = {"pe": block.tensor, "act": block.scalar, "dve": block.vector,
                    "pool": block.gpsimd, "sp": block.sync}
            for e in self.ENG:
                self._emit_engine(deco[e], e)

    def _emit_engine(self, deco, e):
        stream = self.stream[e]
        sems = self.sems

        @deco
        def _(eng):
            for waits, fn, dmainfo in stream:
                for sem, val in waits:
                    eng.wait_ge(sems[sem], val)
                if fn is None:
                    continue
                r = fn(eng)
                if dmainfo is None:
                    r.then_inc(sems[e], 1)
                else:
                    sem, n = dmainfo
                    if not isinstance(r, (list, tuple)):
                        r = [r]
                    assert len(r) == n, (len(r), n)
                    for ins in r:
                        ins.then_inc(sems[sem], 16)


def t5_bucket_np(rel):
    half, max_exact = 16, 8
    rel = np.asarray(rel, np.int64)
    ret = np.where(rel > 0, half, 0)
    n = np.abs(rel)
    nf = np.maximum(n, 1).astype(np.float32)
    large = max_exact + (np.log(nf / np.float32(max_exact)) / np.float32(math.log(128 / max_exact))
                         * np.float32(half - max_exact)).astype(np.int32)
    large = np.minimum(large, half - 1)
    return ret + np.where(n < max_exact, n, large)


def t5_bucket_table(rel):
    return t5_bucket_np(rel)


def build_program(n_layers=DEPTH, tiles=None, do_setup_layer=True, stage=99):
    nc = bass.Bass("TRN2", target_bir_lowering=False)

    def din(name, shape):
        return nc.dram_tensor(name, list(shape), F32, kind="ExternalInput").ap()

    def dout(name, shape):
        return nc.dram_tensor(name, list(shape), F32, kind="ExternalOutput").ap()

    xp = din("xp", [2048, 1024]); xs = din("xs", [128, 1024])
    cbk = din("cbk", [4, 4, 128, 128]); cbv = din("cbv", [4, 4, 128, 128])
    cck = din("cck", [4, 4, 512, 256]); ccv = din("ccv", [4, 4, 512, 256])
    w_in = din("w_in", [4, 1024, 3072]); w_out = din("w_out", [4, 1024, 1024])
    g_pre = din("g_pre", [4, 1024]); g_post = din("g_post", [4, 1024])
    a_norm_g = din("a_norm_g", [4, 256]); a_ws = din("a_ws", [4, 4, 128, 128])
    a_bs = din("a_bs", [4, 4, 128]); b_sinks = din("b_sinks", [4, 8])
    c_rel = din("c_rel", [4, 257, 4]); t5 = din("t5", [32, 8])
    c_ident = din("c_ident", [128, 128]); c_oh = din("c_oh", [32, 255])
    c_maskT = din("c_maskT", [128, 128]); c_rep = din("c_rep", [32, 128])
    c_seq = din("c_seq", [4, 128]); c_mrow = din("c_mrow", [4, 512]); c_sel = din("c_sel", [4, 512])

    yp = dout("yp", [2048, 1024]); ys = dout("ys", [128, 1024])
    bkp = dout("bkp", [4, 128, 128]); bvp = dout("bvp", [4, 128, 128])
    ckp = dout("ckp", [4, 512, 256]); cvp = dout("cvp", [4, 512, 256])
    bks = dout("bks", [4, 128, 128]); bvs = dout("bvs", [4, 128, 128])
    cks = dout("cks", [4, 128, 256]); cvs = dout("cvs", [4, 128, 256])
    avs = dout("avs", [4, 128, 256])

    dbg_hy = dout("dbg_hy", [128, 1024]) if stage == 6 else None
    wbi = nc.dram_tensor("wbi", [DEPTH - 1, 1024, 3072], BF16, kind="Internal")
    wbo = nc.dram_tensor("wbo", [DEPTH - 1, 1024, 1024], BF16, kind="Internal")
    R1_B = nc.dram_tensor("R1_B", [8, 255], F32, kind="Internal")
    R1_C = [nc.dram_tensor(f"R1_C{l}", [4, 639], F32, kind="Internal") for l in range(DEPTH)]
    R_B = nc.dram_tensor("R_B", [8, 64, 255], F32, kind="Internal")
    R_C = [nc.dram_tensor(f"R_C{l}", [4, 64, 639], F32, kind="Internal") for l in range(DEPTH)]

    P = Plan(nc)
    st = contextlib.ExitStack()

    def sb(name, shape, dt=F32):
        return st.enter_context(nc.sbuf_tensor(name, list(shape), dt))

    with st:
        xres = sb("xres", [128, NT, 1024])
        Wi = sb("Wi", [128, 8, 3072], BF16)
        Wo = sb("Wo", [128, 8, 1024], BF16)
        gpost_b = sb("gpost_b", [128, 1024])
        gpre_b = sb("gpre_b", [128, 1024])
        gv_b = sb("gv_b", [128, 256])
        bs_t = sb("bs_t", [128, 16])
        bs_s = sb("bs_s", [128, 16])
        bs_st = sb("bs_st", [16, 128])
        sink_b = sb("sink_b", [128, 32])
        esink = sb("esink", [128, 32])
        t5_sb = sb("t5_sb", [32, 8])
        ident_f = sb("ident_f", [128, 128])
        identb = sb("identb", [128, 128], BF16)
        maskT = sb("maskT", [128, 128])
        rep_f = sb("rep_f", [32, 128])
        repb = sb("repb", [32, 128], BF16)
        seqb = sb("seqb", [4, 128], BF16)
        mrow4b = sb("mrow4b", [4, 512], BF16)
        selb = sb("selb", [4, 512], BF16)
        neghalf = sb("neghalf", [128, 1])
        scratch = sb("scratch", [128, 1280])
        Tq = scratch
        wss_st = sb("wss_st", [128, 4, 128])
        WsT = sb("WsT", [128, 4, 128], BF16)
        WsTs = sb("WsTs", [128, 4, 128], BF16)
        biasTB = [sb(f"biasTB{a}", [128, 2, 512], BF16) for a in range(2)]
        biasTC = [sb(f"biasTC{a}", [128, 5, 256], BF16) for a in range(2)]
        BkT = sb("BkT", [128, 2, 2, 128], BF16)
        CkT = sb("CkT", [128, 5, 2, 128], BF16)
        BV1 = sb("BV1", [128, 2, 2, 65], BF16)
        CV1 = sb("CV1", [128, 5, 4, 65], BF16)
        CV1c = CV1[:, 0:4, :, :]
        kcc = CkT[:, 0:4, :, :].rearrange("p a b c -> p a (b c)")
        CKK = [("CkT", i) for i in range(5)]
        CVK = [("CV1", i) for i in range(5)]
        BkTs = sb("BkTs", [128, 2, 128], BF16)
        CkTs = sb("CkTs", [128, 2, 128], BF16)
        BV1s = sb("BV1s", [128, 2, 65], BF16)
        CV1s = sb("CV1s", [128, 4, 65], BF16)
        kcb = sb("kcb", [128, 256], BF16)
        BkTc = sb("BkTc", [128, 4, 2, 128], BF16)
        BV1c = sb("BV1c", [128, 4, 2, 65], BF16)
        CkTc = sb("CkTc", [128, 2, 512], BF16)
        hy = sb("hy", [128, 1024], BF16)
        hyT = sb("hyT", [128, 8, 128], BF16)
        auav = sb("auav", [128, 512])
        gate = scratch[:, 0:1024]
        kdup = sb("kdup", [128, 256], BF16)
        kc_bf = sb("kc_bf", [128, 256], BF16)
        kvout = sb("kvout", [128, 768])
        hb = sb("hb", [128, 1024], BF16)
        hT = sb("hT", [128, 8, 128], BF16)
        qTB = [sb(f"qTB{i}", [128, 4, 128], BF16) for i in range(2)]
        qTC = [sb(f"qTC{i}", [128, 2, 128], BF16) for i in range(2)]
        vn_bf = sb("vn_bf", [128, 256], BF16)
        mixb = sb("mixb", [128, 256])
        E_C = kvout[0:4, 0:639]
        crelT = auav[0:4, 0:257]
        E_B = auav[0:8, 257:512]
        oh_sb = mixb[0:32, 0:255]
        vn_f = mixb
        PTbuf = sb("PTbuf", [128, 2560], BF16)
        OaccC = sb("OaccC", [128, 260])
        stat = sb("stat", [128, 32])
        den = sb("den", [128, 16])
        rden = sb("rden", [128, 12])

        PS = [st.enter_context(nc.psum_tensor(f"PS{i}", [128, 512], F32)) for i in range(8)]
        P0b = PS[0].bitcast(BF16)

        TQK = ["Tq", ("gate", 0), ("gate", 256), ("gate", 512), ("gate", 768)]

        def r3(ap, a):
            return ap.rearrange("p (a b) -> p a b", a=a)

        def ld(eng, sem, out_ap, in_ap, reads=(), writes=(), batch=False, **kw):
            P.dma(eng, sem, lambda q: [q.dma_start(out=out_ap, in_=in_ap, **kw)], 1,
                  reads=reads, writes=writes, batch=batch)

        def store(out_ap, in_ap, reads, sem="out", batch=False):
            P.dma("pool", sem, lambda q: [q.dma_start(out=out_ap, in_=in_ap)], 1,
                  reads=reads, out=True, batch=batch)

        def cp(eng, out_ap, in_ap, reads, writes):
            if eng == "act":
                P.op("act", lambda a: a.activation(out=out_ap, in_=in_ap, func=AF.Identity), reads, writes)
            else:
                P.op(eng, lambda v: v.tensor_copy(out=out_ap, in_=in_ap), reads, writes)

        def rstd_ops(ss_ap, tmp_ap, out_ap, n, eps, key):
            P.op("dve", lambda v: v.tensor_scalar(out=tmp_ap, in0=ss_ap, scalar1=1.0 / n, scalar2=eps,
                                                  op0=ALU.mult, op1=ALU.add),
                 reads=[key + "ss"], writes=[key + "ms"])
            P.op("pool", lambda g: g.tensor_tensor(out=out_ap, in0=tmp_ap, in1=neghalf[:], op=ALU.pow),
                 reads=[key + "ms", "neghalf"], writes=[key + "rs"])

        for t in range(16):
            ld("sp", "x", xres[:, t, :], xp[t * 128:(t + 1) * 128, :], writes=[("x", t)], batch=True)
        ld("sp", "x", xres[:, 16, :], xs, writes=[("x", 16)], batch=True)
        P.close_batch("x")
        ld("sp", "c", ident_f[:], c_ident, writes=["ident_f"], batch=True)
        ld("sp", "c", maskT[:], c_maskT, writes=["maskT"], batch=True)
        ld("sp", "c", rep_f[:], c_rep, writes=["rep_f"], batch=True)
        ld("sp", "c", oh_sb, c_oh, writes=["oh_sb", "mixb"], batch=True)
        ld("sp", "c", t5_sb[:], t5, writes=["t5_sb"], batch=True)
        ld("sp", "c", bs_st[:], a_bs.rearrange("l g t -> (l g) t"), writes=["bs_st"], batch=True)
        ld("sp", "c", sink_b[:], b_sinks.rearrange("l h -> (l h)").partition_broadcast(128),
           writes=["sink_b"], batch=True)
        P.close_batch("c")
        ld("pool", "cp", seqb[:], c_seq, writes=["seqb"], batch=True)
        ld("pool", "cp", mrow4b[:], c_mrow, writes=["mrow4b"], batch=True)
        ld("pool", "cp", selb[:], c_sel, writes=["selb"], batch=True)
        P.close_batch("cp")

        P.op("dve", lambda v: v.tensor_copy(out=identb[:], in_=ident_f[:]), ["ident_f"], ["identb"])
        P.op("dve", lambda v: v.tensor_copy(out=repb[:], in_=rep_f[:]), ["rep_f"], ["repb"])
        P.op("pool", lambda g: g.memset(neghalf[:], -0.5), [], ["neghalf"])
        P.op("pool", lambda g: g.memset(Tq[:], 0.0), [], TQK)
        P.op("pool", lambda g: g.memset(wss_st[:], 0.0), [], ["wss_st"])
        for i in range(2):
            P.op("pool", (lambda g, i=i: g.memset(qTB[i][:], 0.0)), [], ["qTB"])
            P.op("pool", (lambda g, i=i: g.memset(qTC[i][:], 0.0)), [], ["qTC"])
        for nme, tl in (("BV1", BV1), ("CV1", CV1), ("BV1s", BV1s), ("CV1s", CV1s),
                        ("BV1c", BV1c)):
            P.op("pool", (lambda g, tl=tl: g.memset(tl[:], 1.0)), [], [nme])
        P.op("act", lambda a: a.activation(out=esink[:], in_=sink_b[:], func=AF.Exp), ["sink_b"], ["esink"])

        P.op("pe", lambda pe: pe.transpose(out=PS[2][:, 0:16], in_=bs_st[:], identity=ident_f[0:16, 0:16]),
             ["bs_st", "ident_f"], ["PS2"])
        cp("dve", bs_t[:], PS[2][:, 0:16], ["PS2"], ["bs_t"])
        P.op("pe", lambda pe: pe.matmul(PS[1][:, 0:16], lhsT=rep_f[:], rhs=bs_t[0:32, :], start=True, stop=True),
             ["rep_f", "bs_t"], ["PS1"])
        cp("dve", bs_s[:], PS[1][:, 0:16], ["PS1"], ["bs_s"])

        def build_bias(E_ap, H, W, Rt, HP, L, NTL, dest, destkey, layer_tag, perm=False, ekeys=(), R1t=None):
            rkey = "R" + layer_tag
            P.dma("sp", "rw", lambda q: [q.dma_start(out=R1t.ap(), in_=E_ap)], 1,
                  reads=list(ekeys), writes=[rkey + "1"])
            P.dma("sp", "rw2", lambda q: [q.dma_start(
                out=Rt.ap(), in_=R1t.ap().unsqueeze(1).broadcast_to([H, 64, W]))], 1,
                reads=[rkey + "1"], writes=[rkey])
            width = 1024 // HP if HP == 4 else 640
            tqv = r3(Tq[:, 0:HP * width], HP)
            for A in range(2):
                off = 64 * A
                P.op("pool", lambda g: g.memset(Tq[:], NEG), [], TQK)

                def rd(q, off=off):
                    res = []
                    for hh in range(2):
                        src = bass.AP(Rt, hh * 64 * W + 63, [[W - 1, 64], [2 * 64 * W, HP], [1, L]])
                        res.append(q.dma_start(out=tqv[hh * 64:(hh + 1) * 64, :, off:off + L], in_=src))
                    return res
                P.dma("sp", "bias", rd, 2, reads=[rkey], writes=TQK)
                for j in range(NTL):
                    bank = 1 + (j % 2)

                    def tr(pe, j=j, bank=bank):
                        r = None
                        for hp in range(HP):
                            r = pe.transpose(out=PS[bank][:, hp * 128:(hp + 1) * 128],
                                             in_=tqv[:, hp, j * 128:(j + 1) * 128], identity=ident_f[:])
                        return r
                    P.op("pe", tr, TQK + ["ident_f"], [f"PS{bank}"])
                    if not perm:
                        cp("dve", dest[A][:, j, :], PS[bank][:, 0:HP * 128], [f"PS{bank}"], [destkey])
                    else:
                        for kh in range(2):
                            for hh in range(2):
                                o0 = kh * 256 + hh * 128
                                cp("dve", r3(dest[A][:, j, o0:o0 + 128], 2),
                                   r3(PS[bank][:, kh * 256:(kh + 1) * 256], 2)[:, :, hh * 64:(hh + 1) * 64],
                                   [f"PS{bank}"], [destkey])

        P.op("pe", lambda pe: pe.matmul(PS[3][0:8, 0:255], lhsT=t5_sb[:], rhs=oh_sb, start=True, stop=True),
             ["t5_sb", "oh_sb", "mixb"], ["PS3"])
        cp("dve", E_B, PS[3][0:8, 0:255], ["PS3"], ["EB", "auav"])
        build_bias(E_B, 8, 255, R_B, 4, 192, 2, biasTB, "biasTB", "B", perm=True, ekeys=["EB", "auav"], R1t=R1_B)

        def head_region(idx):
            if idx < 7:
                return 7, idx * 65
            return 3, (idx - 7) * 65

        wloaded = set()

        precast = []
        for ll in range(1, DEPTH):
            for kc in range(8):
                precast.append(("i", ll, kc))
            for kc in range(8):
                precast.append(("o", ll, kc))

        def issue_precast(n):
            for _ in range(n):
                if not precast:
                    break
                kind, ll, kc = precast.pop(0)
                if kind == "i":
                    P.dma("pool", "pc", (lambda q, ll=ll, kc=kc: [q.dma_start(
                        out=wbi.ap()[ll - 1, kc * 128:(kc + 1) * 128, :],
                        in_=w_in[ll, kc * 128:(kc + 1) * 128, :])]), 1, writes=[("wbi", ll)], batch=True)
                else:
                    P.dma("pool", "pc", (lambda q, ll=ll, kc=kc: [q.dma_start(
                        out=wbo.ap()[ll - 1, kc * 128:(kc + 1) * 128, :],
                        in_=w_out[ll, kc * 128:(kc + 1) * 128, :])]), 1, writes=[("wbo", ll)], batch=True)
            if not precast:
                P.close_batch("pc")

        def load_wi(l):
            if l == 0 or n_layers != DEPTH or tiles is not None:
                wsem = f"w{l % 2}"
                for kc in range(8):
                    P.dma("pool", wsem, (lambda q, kc=kc: [q.dma_start(
                        out=Wi[:, kc, :], in_=w_in[l, kc * 128:(kc + 1) * 128, :])]), 1, writes=["Wi"], batch=True)
                P.close_batch(wsem)
            else:
                wsem = f"wb{l % 2}"
                for kc in range(8):
                    P.dma("sp", wsem, (lambda q, kc=kc: [q.dma_start(
                        out=Wi[:, kc, :], in_=wbi.ap()[l - 1, kc * 128:(kc + 1) * 128, :])]), 1,
                        reads=[("wbi", l)], writes=["Wi"], batch=True)
                P.close_batch(wsem)

        def load_wo(l):
            if l == 0 or n_layers != DEPTH or tiles is not None:
                wsem = f"wo{l % 2}"
                for kc in range(8):
                    P.dma("pool", wsem, (lambda q, kc=kc: [q.dma_start(
                        out=Wo[:, kc, :], in_=w_out[l, kc * 128:(kc + 1) * 128, :])]), 1, writes=["Wo"], batch=True)
                P.close_batch(wsem)
            else:
                wsem = f"wbo{l % 2}"
                for kc in range(8):
                    P.dma("sp", wsem, (lambda q, kc=kc: [q.dma_start(
                        out=Wo[:, kc, :], in_=wbo.ap()[l - 1, kc * 128:(kc + 1) * 128, :])]), 1,
                        reads=[("wbo", l)], writes=["Wo"], batch=True)
                P.close_batch(wsem)

        def layer_setup(l):
            if l not in wloaded:
                load_wi(l)
                load_wo(l)
            psem = f"par{l % 2}"
            ws_st = r3(Tq[:, 0:512], 4)
            ld("sp", psem, gpost_b[:], g_post[l, :].partition_broadcast(128), writes=["gpost_b"], batch=True)
            ld("sp", psem, gpre_b[:], g_pre[l, :].partition_broadcast(128), writes=["gpre_b"], batch=True)
            ld("sp", psem, gv_b[:], a_norm_g[l, :].partition_broadcast(128), writes=["gv_b"], batch=True)
            for h in range(4):
                ld("sp", psem, crelT[h:h + 1, :], c_rel[l, :, h].unsqueeze(0), writes=["crelT", "auav"], batch=True,
                   allow_slow_non_contiguous=True)
            ld("sp", psem, ws_st, a_ws[l].rearrange("g t s -> t g s"), writes=TQK, batch=True)
            for j in range(4):
                ld("sp", psem, wss_st[32 * j:32 * j + 32, :, 32 * j:32 * j + 32],
                   a_ws[l, :, 0:32, 0:32].rearrange("g t s -> t g s"), writes=["wss_st"], batch=True)
            P.close_batch(psem)

            def trw(pe, src=ws_st):
                r = None
                for g in range(4):
                    r = pe.transpose(out=PS[1][:, g * 128:(g + 1) * 128], in_=src[:, g, :], identity=ident_f[:])
                return r
            P.op("pe", trw, TQK + ["ident_f"], ["PS1"])
            P.op("dve", lambda v: v.tensor_tensor(out=WsT[:], in0=r3(PS[1][:, :], 4),
                                                  in1=maskT[:].unsqueeze(1).broadcast_to([128, 4, 128]),
                                                  op=ALU.mult), ["PS1", "maskT"], ["WsT"])

            def trws(pe):
                r = None
                for g in range(4):
                    r = pe.transpose(out=PS[2][:, g * 128:(g + 1) * 128], in_=wss_st[:, g, :], identity=ident_f[:])
                return r
            P.op("pe", trws, ["wss_st", "ident_f"], ["PS2"])
            P.op("dve", lambda v: v.tensor_tensor(out=WsTs[:], in0=r3(PS[2][:, :], 4),
                                                  in1=maskT[:].unsqueeze(1).broadcast_to([128, 4, 128]),
                                                  op=ALU.mult), ["PS2", "maskT"], ["WsTs"])
            ECK = ["EC", "kvoutB", "kvoutC"]
            P.op("dve", lambda v: v.tensor_copy(out=E_C[:, 447:639], in_=crelT[:, 0:192]), ["crelT", "auav"], ECK)
            P.op("dve", lambda v: v.tensor_copy(out=E_C[:, 0:447], in_=crelT[:, 0:1].broadcast_to([4, 447])),
                 ["crelT", "auav"] + ECK, ECK)
            build_bias(E_C, 4, 639, R_C[l], 2, 576, 5, biasTC, "biasTC", "C", ekeys=ECK, R1t=R1_C[l])
            for s in range(4):
                P.dma("pool", "bv1c", (lambda q, s=s: [q.dma_start(
                    out=BV1c[:, s, :, 0:64], in_=cbv[l, s].rearrange("p (k d) -> p k d", k=2))]),
                    1, writes=["BV1c"], batch=True)
            P.close_batch("bv1c")
            for s in range(4):
                P.dma("pool", "kcb", (lambda q, s=s: [q.dma_start(
                    out=kcb[:].rearrange("p (k u d) -> p k u d", k=2, u=2),
                    in_=cbk[l, s].rearrange("p (k d) -> p k d", k=2).unsqueeze(2).broadcast_to([128, 2, 2, 64]))]),
                    1, writes=["kcb"])

                def trk(pe):
                    pe.transpose(out=P0b[:, 0:128], in_=kcb[:, 0:128], identity=identb[:])
                    return pe.transpose(out=P0b[:, 128:256], in_=kcb[:, 128:256], identity=identb[:])
                P.op("pe", trk, ["kcb", "identb"], ["PS0"])
                cp("dve", BkTc[:, s, :, :], r3(P0b[:, 0:256], 2), ["PS0"], ["BkTc"])

        prenormed = set()

        def prenorm_act(l, t):
            xt = xres[:, t, :]
            xk = ("x", t)
            P.op("act", lambda a: a.activation(out=hb[:], in_=xt, func=AF.Square, accum_out=stat[:, 0:1]),
                 [xk], ["hb", "n1ss"])
            rstd_ops(stat[:, 0:1], stat[:, 1:2], stat[:, 2:3], 1024, EPS, "n1")
            P.op("dve", lambda v: v.scalar_tensor_tensor(out=hb[:], in0=xt, scalar=stat[:, 2:3], in1=gpre_b[:],
                                                         op0=ALU.mult, op1=ALU.mult),
                 [xk, "n1rs", "gpre_b"], ["hb"])

        def prenorm_pe(l, t):
            def tr_h(pe):
                r = None
                for k in range(8):
                    r = pe.transpose(out=P0b[:, k * 128:(k + 1) * 128], in_=hb[:, k * 128:(k + 1) * 128],
                                     identity=identb[:])
                return r
            P.op("pe", tr_h, ["hb", "identb"], ["PS0"])
            cp("dve", hT[:], r3(P0b[:, :], 8), ["PS0"], ["hT"])

        hoistedG = set()
        pending_tail = []

        def tm_group_g(bank, c0, c1, o0):
            def f(pe):
                r = None
                for k in range(8):
                    r = pe.matmul(PS[bank][:, o0:o0 + (c1 - c0)], lhsT=hT[:, k, :], rhs=Wi[:, k, c0:c1],
                                  start=(k == 0), stop=(k == 7))
                return r
            return f

        def g0_pe(l, t, bank):
            P.op("pe", tm_group_g(bank, 0, 512, 0), ["hT", "Wi"], [f"PS{bank}"])

        def g0_evac(l, t, bank):
            sample = (t == 16)
            cp("dve", auav[:], PS[bank][:, :], [f"PS{bank}"], ["auav"])
            P.op("act", lambda a: a.activation(out=vn_bf[:], in_=auav[:, 256:512], func=AF.Square,
                                               accum_out=stat[:, 3:4]), ["auav"], ["vn_bf", "n2ss"])
            rstd_ops(stat[:, 3:4], stat[:, 4:5], stat[:, 5:6], 256, EPS, "n2")
            if sample:
                P.op("dve", lambda v: v.scalar_tensor_tensor(
                    out=vn_f[:], in0=auav[:, 256:512], scalar=stat[:, 5:6], in1=gv_b[:],
                    op0=ALU.mult, op1=ALU.mult), ["auav", "n2rs", "gv_b"], ["mixb"])
                cp("dve", vn_bf[:], vn_f[:], ["mixb"], ["vn_bf"])
                store(avs[l], vn_f[:], ["mixb"], sem="o_av")
            else:
                P.op("dve", lambda v: v.scalar_tensor_tensor(
                    out=vn_bf[:], in0=auav[:, 256:512], scalar=stat[:, 5:6], in1=gv_b[:],
                    op0=ALU.mult, op1=ALU.mult), ["auav", "n2rs", "gv_b"], ["vn_bf"])

        def g4_pe(l, t, bank):
            P.op("pe", tm_group_g(bank, 2304, 2816, 0), ["hT", "Wi"], [f"PS{bank}"])

        def g4_evac(l, t, bank):
            sample = (t == 16)
            if sample:
                cv_dst, cvkey = CV1s[:, :, 0:64], "CV1s"
            else:
                cv_dst, cvkey = CV1[:, t % 5, :, 0:64], ("CV1", t % 5)
            cp("dve", kc_bf[:], PS[bank][:, 0:256], [f"PS{bank}"], ["kc_bf"])
            cp("dve", cv_dst, r3(PS[bank][:, 256:512], 4), [f"PS{bank}"], [cvkey])
            if sample or t >= 12:
                cp("dve", kvout[:, 256:768], PS[bank][:, :], [f"PS{bank}"], ["kvoutC"])

        def tile_ops(l, t):
            sample = (t == 16)
            last = (l == n_layers - 1)
            xt = xres[:, t, :]
            xk = ("x", t)
            out_bkv = sample or t == 15
            out_ckv = sample or t >= 12

            if (l, t) not in prenormed:
                prenorm_act(l, t)
                prenorm_pe(l, t)
            if stage < 2:
                return
            def tm_group(bank, c0, c1, o0):
                def f(pe):
                    r = None
                    for k in range(8):
                        r = pe.matmul(PS[bank][:, o0:o0 + (c1 - c0)], lhsT=hT[:, k, :], rhs=Wi[:, k, c0:c1],
                                      start=(k == 0), stop=(k == 7))
                    return r
                return f

            def fm_blocks(bank, c0, nb):
                def f(pe):
                    r = None
                    for j in range(nb):
                        for k in range(8):
                            r = pe.matmul(PS[bank][:, j * 128:(j + 1) * 128],
                                          lhsT=Wi[:, k, c0 + j * 128:c0 + (j + 1) * 128], rhs=hT[:, k, :],
                                          start=(k == 0), stop=(k == 7))
                    return r
                return f

            def gate_ops(bank, p0, width, g0):
                P.op("act", lambda a: a.activation(out=gate[:, g0:g0 + width], in_=PS[bank][:, p0:p0 + width],
                                                   func=AF.Tanh, scale=0.5),
                     [f"PS{bank}"], [("gate", g0)])
                P.op("dve", lambda v: v.scalar_tensor_tensor(
                    out=gate[:, g0:g0 + width], in0=gate[:, g0:g0 + width], scalar=1.0,
                    in1=PS[bank][:, p0:p0 + width], op0=ALU.add, op1=ALU.mult),
                    [f"PS{bank}", ("gate", g0)], [("gate", g0)])

            rin = ["hT", "Wi"]
            if (l, t) not in hoistedG:
                g0_pe(l, t, 1)
                g0_evac(l, t, 1)
            P.op("pe", tm_group(3, 1280, 1792, 0), rin, ["PS3"])
            slotB = t % 2
            slotC = t % 5
            if sample:
                bv_dst, bvkey = BV1s[:, :, 0:64], "BV1s"
                cv_dst, cvkey = CV1s[:, :, 0:64], "CV1s"
            else:
                bv_dst, bvkey = BV1[:, slotB, :, 0:64], ("BV1", slotB)
                cv_dst, cvkey = CV1[:, slotC, :, 0:64], ("CV1", slotC)
            P.op("act", lambda a: a.activation(out=gate[:, 256:512], in_=PS[3][:, 256:512], func=AF.Tanh, scale=0.5),
                 ["PS3"], [("gate", 256)])
            P.op("dve", lambda v: v.tensor_copy(
                out=kdup[:].rearrange("p (k u d) -> p k u d", k=2, u=2),
                in_=r3(PS[3][:, 0:128], 2).unsqueeze(2).broadcast_to([128, 2, 2, 64])),
                ["PS3", ("gate", 256)], ["kdup"])
            cp("dve", bv_dst, r3(PS[3][:, 128:256], 2), ["PS3"], [bvkey])
            P.op("dve", lambda v: v.scalar_tensor_tensor(
                out=gate[:, 256:512], in0=gate[:, 256:512], scalar=1.0, in1=PS[3][:, 256:512],
                op0=ALU.add, op1=ALU.mult), ["PS3", ("gate", 256)], [("gate", 256)])
            if out_bkv:
                cp("dve", kvout[:, 0:256], PS[3][:, 0:256], ["PS3"], ["kvoutB"])
            P.op("pe", tm_group(2, 512, 768, 0), rin, ["PS2"])
            P.op("pe", tm_group(2, 1792, 2048, 256), rin, ["PS2"])
            g13 = r3(gate[:, 0:1024], 2)[:, :, 0:256]
            P.op("act", lambda a: a.activation(out=g13, in_=r3(PS[2][:, :], 2), func=AF.Tanh, scale=0.5),
                 ["PS2"], [("gate", 0), ("gate", 512)])
            P.op("dve", lambda v: v.scalar_tensor_tensor(out=g13, in0=g13, scalar=1.0, in1=r3(PS[2][:, :], 2),
                                                         op0=ALU.add, op1=ALU.mult),
                 ["PS2", ("gate", 0), ("gate", 512)], [("gate", 0), ("gate", 512)])
            if (l, t) not in hoistedG:
                g4_pe(l, t, 1)
                g4_evac(l, t, 1)
            P.op("pe", tm_group(1, 2816, 3072, 0), rin, ["PS1"])
            gate_ops(1, 0, 256, 768)
            def tr_k(pe):
                pe.transpose(out=P0b[:, 0:128], in_=kdup[:, 0:128], identity=identb[:])
                pe.transpose(out=P0b[:, 128:256], in_=kdup[:, 128:256], identity=identb[:])
                pe.transpose(out=P0b[:, 256:384], in_=kc_bf[:, 0:128], identity=identb[:])
                return pe.transpose(out=P0b[:, 384:512], in_=kc_bf[:, 128:256], identity=identb[:])
            P.op("pe", tr_k, ["kdup", "kc_bf", "identb"], ["PS0"])
            if sample:
                cp("dve", BkTs[:], r3(P0b[:, 0:256], 2), ["PS0"], ["BkTs"])
                cp("dve", CkTs[:], r3(P0b[:, 256:512], 2), ["PS0"], ["CkTs"])
            else:
                cp("dve", BkT[:, slotB, :, :], r3(P0b[:, 0:256], 2), ["PS0"], [("BkT", slotB)])
                cp("dve", CkT[:, slotC, :, :], r3(P0b[:, 256:512], 2), ["PS0"], [("CkT", slotC)])

            while pending_tail:
                pending_tail.pop(0)()
            P.op("pe", fm_blocks(4, 768, 4), rin, ["PS4"])
            for i in range(2):
                P.op("act", (lambda a, i=i: a.activation(out=qTB[i][i * 64:(i + 1) * 64], in_=r3(PS[4][i * 64:(i + 1) * 64, :], 4),
                                                        func=AF.Identity, scale=0.125)), ["PS4"], ["qTB"])
            P.op("pe", fm_blocks(3, 2048, 2), rin, ["PS3"])
            for i in range(2):
                P.op("act", (lambda a, i=i: a.activation(out=qTC[i][i * 64:(i + 1) * 64], in_=r3(PS[3][i * 64:(i + 1) * 64, 0:256], 2),
                                                        func=AF.Identity, scale=0.125)), ["PS3"], ["qTC"])

            if stage < 3:
                return
            last_tile = (t == (NT - 1 if tiles is None else list(tiles)[-1]))
            hoist = t + 1 < NT and (tiles is None or (t + 1) in tiles)
            if hoist:
                prenormed.add((l, t + 1))
                prenorm_act(l, t + 1)
            if l == 0 and n_layers == DEPTH and tiles is None:
                issue_precast(4 if t < 11 else 48)
            def kvstore(o, i, r):
                store(o, i, r, sem=f"o_kv{t % 2}", batch=True)
            if out_bkv:
                if sample:
                    kvstore(bks[l], kvout[:, 0:128], ["kvoutB"])
                    kvstore(bvs[l], kvout[:, 128:256], ["kvoutB"])
                else:
                    kvstore(bkp[l], kvout[:, 0:128], ["kvoutB"])
                    kvstore(bvp[l], kvout[:, 128:256], ["kvoutB"])
            if out_ckv:
                if sample:
                    kvstore(cks[l], kvout[:, 256:512], ["kvoutC"])
                    kvstore(cvs[l], kvout[:, 512:768], ["kvoutC"])
                else:
                    r0 = (t - 12) * 128
                    kvstore(ckp[l, r0:r0 + 128, :], kvout[:, 256:512], ["kvoutC"])
                    kvstore(cvp[l, r0:r0 + 128, :], kvout[:, 512:768], ["kvoutC"])
            P.close_batch(f"o_kv{t % 2}")

            if stage < 4:
                return
            SUB = 99
            if stage == 4 and SUB < 1:
                return
            wst, wkey = (WsTs, "WsTs") if sample else (WsT, "WsT")
            bsx = bs_s if sample else bs_t

            def amix(pe):
                r = None
                for g in range(4):
                    r = pe.matmul(PS[4][:, g * 64:(g + 1) * 64], lhsT=wst[:, g, :], rhs=vn_bf[:, g * 64:(g + 1) * 64],
                                  start=True, stop=True)
                return r
            P.op("pe", amix, [wkey, "vn_bf"], ["PS4"])
            if stage == 4 and SUB < 2:
                return
            P.op("dve", lambda v: v.tensor_tensor(
                out=r3(mixb[:], 4), in0=r3(PS[4][:, 0:256], 4),
                in1=bsx[:, l * 4:(l + 1) * 4].unsqueeze(2).broadcast_to([128, 4, 64]), op=ALU.add),
                ["PS4", "bs_t", "bs_s"], ["mixb"])
            P.op("dve", lambda v: v.tensor_tensor(out=mixb[:], in0=mixb[:], in1=auav[:, 0:256], op=ALU.mult),
                 ["mixb", "auav"], ["mixb"])
            P.op("dve", lambda v: v.tensor_tensor(out=hy[:, 0:256], in0=mixb[:], in1=gate[:, 0:256], op=ALU.mult),
                 ["mixb", ("gate", 0)], ["hy"])

            if stage < 5:
                return
            stctr = [0]

            def emit_st_exp(g, slot):
                base = slot * 1280
                nc_ = g["ncols"]
                for i, kt in enumerate(g["kts"]):
                    bank = (1, 2, 5, 6)[stctr[0] % 4]
                    stctr[0] += 1
                    p0, plen = kt["p0"], kt["plen"]
                    P.op("pe", (lambda pe, kt=kt, bank=bank: kt["st"](pe, bank)), kt["reads"], [f"PS{bank}"])
                    P.op("act", (lambda a, bank=bank, p0=p0, plen=plen, c0=base + i * nc_:
                                 a.activation(out=PTbuf[p0:p0 + plen, c0:c0 + nc_],
                                              in_=PS[bank][p0:p0 + plen, 0:nc_], func=AF.Exp)),
                         [f"PS{bank}"], [("PT", slot)])

            def emit_pv(g, slot):
                base = slot * 1280
                nc_ = g["ncols"]
                w = g["width"]
                kts = g["kts"]
                n = len(kts)

                def pv(pe):
                    r = None
                    for hl, (hidx, cb) in enumerate(g["heads"]):
                        ob, oc = head_region(hidx)
                        if sample:
                            oap = PS[ob][:, oc:oc + 65]
                        else:
                            oap = PS[ob][g["hf"] * 64:(g["hf"] + 1) * 64, oc:oc + 65]
                        for i, kt in enumerate(kts):
                            p0, plen = kt["p0"], kt["plen"]
                            c0 = base + i * nc_ + cb * w
                            r = pe.matmul(oap, lhsT=PTbuf[p0:p0 + plen, c0:c0 + w], rhs=kt["vrhs"](hl),
                                          start=(i == 0), stop=(i == n - 1))
                    return r
                vreads = sorted({k for kt in kts for k in kt["vreads"]}, key=str)
                obanks = sorted({f"PS{head_region(h)[0]}" for h, _ in g["heads"]})
                P.op("pe", pv, [("PT", slot)] + vreads, obanks)
                if g.get("acc") is not None:
                    first = g["acc"]
                    if first:
                        P.op("dve", lambda v: v.tensor_copy(out=OaccC[:], in_=PS[3][:, 65:325]), ["PS3"], ["OaccC"])
                    else:
                        P.op("dve", lambda v: v.tensor_tensor(out=OaccC[:], in0=OaccC[:], in1=PS[3][:, 65:325],
                                                              op=ALU.add), ["PS3", "OaccC"], ["OaccC"])

            def run_groups(groups, nslots):
                prev = None
                for gi, g in enumerate(groups):
                    slot = gi % nslots
                    emit_st_exp(g, slot)
                    if prev is not None and nslots > 1:
                        emit_pv(*prev)
                        prev = None
                    if nslots == 1:
                        emit_pv(g, slot)
                    else:
                        prev = (g, slot)
                if prev is not None:
                    emit_pv(*prev)

            groups = []
            if not sample:
                for hf in range(2):
                    c = 2 * t + hf
                    q0 = hf * 64
                    for kind in "BC":
                        n_prev = 2 if kind == "B" else 8
                        ts = (c - n_prev) // 2
                        ktiles = {}
                        for cc in range(c - n_prev, c + 1):
                            if cc >= 0:
                                ktiles.setdefault(cc // 2, []).append(cc % 2)
                        ktl = []
                        for tt in sorted(ktiles):
                            ktl.append((tt, 0, 128, tt - ts))
                        if kind == "B":
                            for kh in range(2):
                                kts = []
                                for tt, p0, plen, j in ktl:
                                    sl = tt % 2

                                    def st_fn(pe, bank, j=j, kh=kh, p0=p0, plen=plen, hf=hf, sl=sl, q0=q0):
                                        DST = "bq"
                                        r = None
                                        if "b" in DST:
                                            r = pe.matmul(PS[bank][p0:p0 + plen, 0:256],
                                                          lhsT=identb[p0:p0 + plen, p0:p0 + plen],
                                                          rhs=biasTB[hf][p0:p0 + plen, j, kh * 256:(kh + 1) * 256],
                                                          start=True, stop=("q" not in DST))
                                        for hh in (range(2) if "q" in DST else []):
                                            r = pe.matmul(PS[bank][p0:p0 + plen, hh * 128:(hh + 1) * 128],
                                                          lhsT=BkT[:, sl, kh, p0:p0 + plen],
                                                          rhs=qTB[hh][:, 2 * kh:2 * kh + 2, q0:q0 + 64],
                                                          start=("b" not in DST and hh == 0), stop=(hh == 1))
                                        return r
                                    kts.append(dict(p0=p0, plen=plen, st=st_fn,
                                                    reads=["biasTB", "identb", ("BkT", sl), "qTB"],
                                                    vreads=[("BV1", sl)],
                                                    vrhs=(lambda hl, p0=p0, plen=plen, sl=sl, kh=kh:
                                                          BV1[p0:p0 + plen, sl, kh, 0:65])))
                                groups.append(dict(kts=kts, ncols=256, width=64, hf=hf,
                                                   heads=[(4 * kh + hg, (hg % 2) * 2 + hg // 2) for hg in range(4)]))
                        else:
                            kts = []
                            for tt, p0, plen, j in ktl:
                                sl = tt % 5

                                def st_fn(pe, bank, j=j, p0=p0, plen=plen, hf=hf, sl=sl, q0=q0):
                                    pe.matmul(PS[bank][p0:p0 + plen, 0:256],
                                              lhsT=identb[p0:p0 + plen, p0:p0 + plen],
                                              rhs=biasTC[hf][p0:p0 + plen, j, :], start=True, stop=False)
                                    r = None
                                    for h in range(4):
                                        b0 = (h % 2) * 64
                                        r = pe.matmul(PS[bank][p0:p0 + plen, h * 64:(h + 1) * 64],
                                                      lhsT=CkT[:, sl, h // 2, p0:p0 + plen],
                                                      rhs=qTC[h % 2][:, h // 2, q0:q0 + 64],
                                                      start=False, stop=(h == 3))
                                    return r
                                kts.append(dict(p0=p0, plen=plen, st=st_fn,
                                                reads=["biasTC", "identb", ("CkT", sl), "qTC"],
                                                vreads=[("CV1", sl)],
                                                vrhs=(lambda hl, p0=p0, plen=plen, sl=sl:
                                                      CV1[p0:p0 + plen, sl, hl, 0:65])))
                            groups.append(dict(kts=kts, ncols=256, width=64, hf=hf,
                                               heads=[(8 + h, h) for h in range(4)]))
                run_groups(groups, 2)
            else:
                for kh in range(2):
                    kts = []
                    for s in range(4):
                        def st_fn(pe, bank, kh=kh, s=s):
                            rb = r3(biasTB[0][:, 0, kh * 256:(kh + 1) * 256], 4)[:, :, 0:32]
                            pe.matmul(PS[bank][:, :], lhsT=identb[:],
                                      rhs=rb.unsqueeze(2).broadcast_to([128, 4, 4, 32]), start=True, stop=False)
                            pe.matmul(PS[bank][:, :], lhsT=selb[:, s * 128:(s + 1) * 128], rhs=mrow4b[:],
                                      start=False, stop=False)
                            r = None
                            for hh in range(2):
                                r = pe.matmul(PS[bank][:, hh * 256:(hh + 1) * 256],
                                              lhsT=BkTc[:, s, kh, :],
                                              rhs=qTB[hh][:, 2 * kh:2 * kh + 2, :],
                                              start=False, stop=(hh == 1))
                            return r
                        kts.append(dict(p0=0, plen=128, st=st_fn,
                                        reads=["biasTB", "identb", "selb", "mrow4b", "BkTc", "qTB"],
                                        vreads=["BV1c"], vrhs=(lambda hl, s=s, kh=kh: BV1c[:, s, kh, 0:65])))

                    def st_fn(pe, bank, kh=kh):
                        rb = r3(biasTB[0][0:32, 1, kh * 256:(kh + 1) * 256], 4)[:, :, 0:32]
                        pe.matmul(PS[bank][:, :], lhsT=repb[:],
                                  rhs=rb.unsqueeze(2).broadcast_to([32, 4, 4, 32]), start=True, stop=False)
                        pe.matmul(PS[bank][:, :], lhsT=seqb[:], rhs=mrow4b[:], start=False, stop=False)
                        r = None
                        for hh in range(2):
                            r = pe.matmul(PS[bank][:, hh * 256:(hh + 1) * 256],
                                          lhsT=BkTs[:, kh, :],
                                          rhs=qTB[hh][:, 2 * kh:2 * kh + 2, :],
                                          start=False, stop=(hh == 1))
                        return r
                    kts.append(dict(p0=0, plen=128, st=st_fn,
                                    reads=["biasTB", "repb", "seqb", "mrow4b", "BkTs", "qTB"],
                                    vreads=["BV1s"], vrhs=(lambda hl, kh=kh: BV1s[:, kh, 0:65])))
                    run_groups([dict(kts=kts, ncols=512, width=128, hf=0,
                                     heads=[(4 * kh + hg, (hg % 2) * 2 + hg // 2) for hg in range(4)])], 1)
                for s in range(4):
                    P.dma("pool", "kcc", (lambda q, s=s: [q.dma_start(
                        out=kcc, in_=cck[l, s].rearrange("(kt p) c -> p kt c", p=128))]), 1, writes=CKK)
                    P.dma("pool", "cv1c", (lambda q, s=s: [q.dma_start(
                        out=CV1c[:, kt, :, 0:64],
                        in_=ccv[l, s, kt * 128:(kt + 1) * 128, :].rearrange("p (h d) -> p h d", h=4))
                        for kt in range(4)]), 4, writes=CVK)

                    def trc(pe):
                        r = None
                        for blk in range(2):
                            for kt in range(4):
                                o0 = (blk * 4 + kt) * 128
                                r = pe.transpose(out=P0b[:, o0:o0 + 128], in_=kcc[:, kt, blk * 128:(blk + 1) * 128],
                                                 identity=identb[:])
                        return r
                    P.op("pe", trc, CKK + ["identb"], ["PS0"])
                    cp("dve", CkTc[:], r3(P0b[:, :], 2), ["PS0"], ["CkTc"])
                    kts = []
                    for kt in range(4):
                        def st_fn(pe, bank, kt=kt, s=s):
                            rb = r3(biasTC[0][:, kt, :], 4)[:, :, 0:32]
                            pe.matmul(PS[bank][:, :], lhsT=identb[:],
                                      rhs=rb.unsqueeze(2).broadcast_to([128, 4, 4, 32]), start=True, stop=False)
                            pe.matmul(PS[bank][:, :], lhsT=selb[:, s * 128:(s + 1) * 128], rhs=mrow4b[:],
                                      start=False, stop=False)
                            r = None
                            for h in range(4):
                                b0 = (h % 2) * 64
                                r = pe.matmul(PS[bank][:, h * 128:(h + 1) * 128],
                                              lhsT=CkTc[:, h // 2, kt * 128:(kt + 1) * 128],
                                              rhs=qTC[h % 2][:, h // 2, :], start=False, stop=(h == 3))
                            return r
                        kts.append(dict(p0=0, plen=128, st=st_fn,
                                        reads=["biasTC", "identb", "selb", "mrow4b", "CkTc", "qTC"],
                                        vreads=CVK, vrhs=(lambda hl, kt=kt: CV1c[:, kt, hl, 0:65])))
                    run_groups([dict(kts=kts, ncols=512, width=128, hf=0, acc=(s == 0),
                                     heads=[(8 + h, h) for h in range(4)])], 1)

                def st_fn(pe, bank):
                    rb = r3(biasTC[0][0:32, 4, :], 4)[:, :, 0:32]
                    pe.matmul(PS[bank][:, :], lhsT=repb[:],
                              rhs=rb.unsqueeze(2).broadcast_to([32, 4, 4, 32]), start=True, stop=False)
                    pe.matmul(PS[bank][:, :], lhsT=seqb[:], rhs=mrow4b[:], start=False, stop=False)
                    r = None
                    for h in range(4):
                        b0 = (h % 2) * 64
                        r = pe.matmul(PS[bank][:, h * 128:(h + 1) * 128], lhsT=CkTs[:, h // 2, :],
                                      rhs=qTC[h % 2][:, h // 2, :], start=False, stop=(h == 3))
                    return r
                run_groups([dict(kts=[dict(p0=0, plen=128, st=st_fn,
                                           reads=["biasTC", "repb", "seqb", "mrow4b", "CkTs", "qTC"],
                                           vreads=["CV1s"], vrhs=(lambda hl: CV1s[:, hl, 0:65]))],
                                 ncols=512, width=128, hf=0, acc=False,
                                 heads=[(8 + h, h) for h in range(4)])], 1)

            if stage < 6:
                return
            if last_tile and l + 1 < n_layers:
                wloaded.add(l + 1)
                load_wi(l + 1)
            if hoist:
                prenorm_pe(l, t + 1)
                hoistedG.add((l, t + 1))
                g0_pe(l, t + 1, 1)
                g4_pe(l, t + 1, 2)
            P.op("dve", lambda v: v.tensor_tensor(
                out=den[:, 0:7].unsqueeze(2), in0=r3(PS[7][:, 0:455], 7)[:, :, 64:65],
                in1=esink[:, l * 8:l * 8 + 7].unsqueeze(2), op=ALU.add),
                ["PS7", "esink"], ["den"])
            P.op("dve", lambda v: v.tensor_tensor(
                out=den[:, 7:8], in0=PS[3][:, 64:65], in1=esink[:, l * 8 + 7:l * 8 + 8], op=ALU.add),
                ["PS3", "esink", "den"], ["den"])
            if sample:
                P.op("dve", lambda v: v.tensor_copy(out=den[:, 8:12].unsqueeze(2), in_=r3(OaccC[:], 4)[:, :, 64:65]),
                     ["OaccC", "den"], ["den"])
            else:
                P.op("dve", lambda v: v.tensor_copy(out=den[:, 8:12].unsqueeze(2),
                                                    in_=r3(PS[3][:, 65:325], 4)[:, :, 64:65]),
                     ["PS3", "den"], ["den"])
            P.op("dve", lambda v: v.reciprocal(out=rden[:], in_=den[:, 0:12]), ["den"], ["rden"])
            gk = [("gate", 256), ("gate", 512), ("gate", 768)]
            P.op("dve", lambda v: v.tensor_tensor(out=r3(gate[:, 256:1024], 12), in0=r3(gate[:, 256:1024], 12),
                                                  in1=rden[:].unsqueeze(2).broadcast_to([128, 12, 64]), op=ALU.mult),
                 ["rden"] + gk, gk)
            P.op("dve", lambda v: v.tensor_tensor(out=r3(hy[:, 256:704], 7), in0=r3(PS[7][:, 0:455], 7)[:, :, 0:64],
                                                  in1=r3(gate[:, 256:704], 7), op=ALU.mult), ["PS7"] + gk, ["hy"])
            if sample:
                P.op("dve", lambda v: v.tensor_tensor(out=hy[:, 704:768], in0=PS[3][:, 0:64], in1=gate[:, 704:768],
                                                      op=ALU.mult), ["PS3"] + gk, ["hy"])
                P.op("dve", lambda v: v.tensor_tensor(out=r3(hy[:, 768:1024], 4), in0=r3(OaccC[:], 4)[:, :, 0:64],
                                                      in1=r3(gate[:, 768:1024], 4), op=ALU.mult), ["OaccC"] + gk, ["hy"])
            else:
                P.op("dve", lambda v: v.tensor_tensor(out=r3(hy[:, 704:1024], 5), in0=r3(PS[3][:, 0:325], 5)[:, :, 0:64],
                                                      in1=r3(gate[:, 704:1024], 5), op=ALU.mult), ["PS3"] + gk, ["hy"])

            if stage < 7:
                if stage == 6:
                    store(dbg_hy, hy[:], ["hy"])
                return
            def tr_y(pe):
                r = None
                for k in range(8):
                    r = pe.transpose(out=P0b[:, k * 128:(k + 1) * 128], in_=hy[:, k * 128:(k + 1) * 128],
                                     identity=identb[:])
                return r
            P.op("pe", tr_y, ["hy", "identb"], ["PS0"])
            cp("dve", hyT[:], r3(P0b[:, :], 8), ["PS0"], ["hyT"])
            if hoist:
                g0_evac(l, t + 1, 1)
                g4_evac(l, t + 1, 2)
            for n in range(2):
                def oproj(pe, n=n):
                    r = None
                    for k in range(8):
                        r = pe.matmul(PS[5 + n][:, :], lhsT=hyT[:, k, :], rhs=Wo[:, k, n * 512:(n + 1) * 512],
                                      start=(k == 0), stop=(k == 7))
                    return r
                P.op("pe", oproj, ["hyT", "Wo"], [f"PS{5 + n}"])
            if last_tile and l + 1 < n_layers:
                load_wo(l + 1)
            P.op("act", lambda a: a.activation(out=hy[:, 0:512], in_=PS[5][:, :], func=AF.Square,
                                               accum_out=stat[:, 16:17]), ["PS5"], ["hy", "n3a"])
            P.op("act", lambda a: a.activation(out=hy[:, 512:1024], in_=PS[6][:, :], func=AF.Square,
                                               accum_out=stat[:, 17:18]), ["PS6"], ["hy", "n3b"])
            def tail():
                P.op("dve", lambda v: v.tensor_tensor(out=stat[:, 18:19], in0=stat[:, 16:17], in1=stat[:, 17:18],
                                                      op=ALU.add), ["n3a", "n3b"], ["n3ss"])
                rstd_ops(stat[:, 18:19], stat[:, 19:20], stat[:, 20:21], 1024, 4.0 * EPS, "n3")
                for n in range(2):
                    P.op("dve", (lambda v, n=n: v.scalar_tensor_tensor(
                        out=PS[5 + n][:, :], in0=PS[5 + n][:, :], scalar=stat[:, 20:21],
                        in1=gpost_b[:, n * 512:(n + 1) * 512], op0=ALU.mult, op1=ALU.mult)),
                        [f"PS{5 + n}", "n3rs", "gpost_b"], [f"PS{5 + n}"])
                    P.op("dve", (lambda v, n=n: v.tensor_tensor(
                        out=xt[:, n * 512:(n + 1) * 512], in0=xt[:, n * 512:(n + 1) * 512], in1=PS[5 + n][:, :],
                        op=ALU.add)), [f"PS{5 + n}", xk], [xk])
                if last:
                    if sample:
                        store(ys, xt, [xk])
                    else:
                        store(yp[t * 128:(t + 1) * 128, :], xt, [xk])
            if last_tile:
                tail()
            else:
                pending_tail.append(tail)

        for l in range(n_layers):
            if do_setup_layer:
                layer_setup(l)
            for t in (range(NT) if tiles is None else tiles):
                tile_ops(l, t)
        P.final_wait("pool")
        P.emit()
    return nc


_CACHE = {}


def _consts():
    ident = np.eye(128, dtype=np.float32)
    rel = np.arange(255) - 191
    bucket = t5_bucket_table(rel)
    oh = np.zeros((32, 255), np.float32)
    oh[bucket, np.arange(255)] = 1.0
    s_idx = np.arange(128)[:, None]
    t_idx = np.arange(128)[None, :]
    maskT = (t_idx >= s_idx).astype(np.float32)
    rep = np.zeros((32, 128), np.float32)
    rep[np.arange(128) % 32, np.arange(128)] = 1.0
    seq = np.zeros((4, 128), np.float32)
    seq[np.arange(128) // 32, np.arange(128)] = 1.0
    mrow = np.full((4, 4, 4, 32), NEG, np.float32)
    for s in range(4):
        mrow[s, :, s, :] = 0.0
    sel = np.zeros((4, 4, 128), np.float32)
    for s in range(4):
        sel[s, s, :] = 1.0
    return dict(c_ident=ident, c_oh=oh, c_maskT=maskT, c_rep=rep, c_seq=seq,
                c_mrow=np.ascontiguousarray(mrow.reshape(4, 512)),
                c_sel=np.ascontiguousarray(sel.reshape(4, 512)))


def kernel(x_prompt, x_sample, cache_b_k, cache_b_v, cache_c_k, cache_c_v,
           g_pre, g_post, w_in, w_out, a_norm_g, a_ws, a_bs, b_sinks, c_rel_bias, t5_bias):
    f = lambda a: np.ascontiguousarray(np.asarray(a, dtype=np.float32))
    if "nc" not in _CACHE:
        _CACHE["nc"] = build_program()
    nc = _CACHE["nc"]
    consts = _consts()
    shared = dict(w_in=f(w_in), w_out=f(w_out), g_pre=f(g_pre), g_post=f(g_post), a_norm_g=f(a_norm_g),
                  a_ws=f(a_ws), a_bs=f(a_bs), b_sinks=f(b_sinks), c_rel=f(c_rel_bias), t5=f(t5_bias), **consts)
    x_prompt = f(x_prompt); x_sample = f(x_sample)
    cbk = f(cache_b_k).reshape(4, 32, 128, 128); cbv = f(cache_b_v).reshape(4, 32, 128, 128)
    cck = f(cache_c_k).reshape(4, 32, 512, 256); ccv = f(cache_c_v).reshape(4, 32, 512, 256)
    in_maps = []
    for i in range(8):
        m = dict(shared)
        m["xp"] = x_prompt[i]
        m["xs"] = np.ascontiguousarray(x_sample[4 * i:4 * i + 4].reshape(128, 1024))
        m["cbk"] = np.ascontiguousarray(cbk[:, 4 * i:4 * i + 4])
        m["cbv"] = np.ascontiguousarray(cbv[:, 4 * i:4 * i + 4])
        m["cck"] = np.ascontiguousarray(cck[:, 4 * i:4 * i + 4])
        m["ccv"] = np.ascontiguousarray(ccv[:, 4 * i:4 * i + 4])
        in_maps.append(m)
    res = run_bass_kernel_spmd(nc, in_maps, core_ids=list(range(8)))
    R = res.results
    y_prompt = np.stack([np.asarray(r["yp"]) for r in R], 0)
    y_sample = np.concatenate([np.asarray(r["ys"]).reshape(4, 32, 1024) for r in R], 0)
    def prm(k, rows, h):
        return np.stack([np.asarray(r[k]).reshape(4, rows, h, 64) for r in R], 1)
    def smp(k, h):
        return np.concatenate([np.asarray(r[k]).reshape(4, 4, 32, h, 64) for r in R], 1)
    a_v = np.concatenate([np.asarray(r["avs"]).reshape(4, 4, 32, 256) for r in R], 1)
    return (y_prompt.astype(np.float32), y_sample.astype(np.float32),
            prm("bkp", 128, 2), prm("bvp", 128, 2), prm("ckp", 512, 4), prm("cvp", 512, 4),
            smp("bks", 2), smp("bvs", 2), smp("cks", 4), smp("cvs", 4), a_v)
```
